# Optimizing a Trainium2 kernel written in Bass

```python
import math
import jax, jax.numpy as jnp
from jax import lax
import numpy as np

D_MODEL = 1024
BATCH = 4
SEQ = 8192
DEPTH = 2

MIX_WIDTH = D_MODEL
DN_HEADS = 4
DN_HEAD_DIM = 128
DN_WIDTH = DN_HEADS * DN_HEAD_DIM
DN_CHUNK = 64
CONV_WIDTH = 3
DT_MIN = 1e-3
DT_MAX = 1e-1
SG_GROUPS = 4
SG_WIDTH = MIX_WIDTH - DN_WIDTH
SG_GROUP_DIM = SG_WIDTH // SG_GROUPS
SG_CHUNK = 128
FFN_HIDDEN = -(-8 * D_MODEL // (3 * 256)) * 256
EPS = 1e-6

OFF_Z = 3 * DN_WIDTH
OFF_A = 4 * DN_WIDTH
OFF_B = OFF_A + 2 * DN_HEADS
OFF_SG = OFF_B + 2 * DN_HEADS
PROJ_WIDTH = OFF_SG + 2 * SG_WIDTH

kernel_name = "hymba_gdn_gmlp_bidir_encoder"


def rms_norm(x, gain):
    xf = x.astype(jnp.float32)
    y = xf * lax.rsqrt(jnp.mean(xf * xf, axis=-1, keepdims=True) + EPS)
    return (y * gain.astype(jnp.float32)).astype(x.dtype)


def layer_norm(x, gain, bias):
    xf = x.astype(jnp.float32)
    mu = jnp.mean(xf, axis=-1, keepdims=True)
    xc = xf - mu
    y = xc * lax.rsqrt(jnp.mean(xc * xc, axis=-1, keepdims=True) + EPS)
    return (y * gain.astype(jnp.float32) + bias.astype(jnp.float32)).astype(x.dtype)


def l2_normalize(t):
    return t * lax.rsqrt(jnp.sum(t * t, axis=-1, keepdims=True) + EPS)


def centred_depthwise_conv(x, w):
    k_width, channels = w.shape
    pad = (k_width - 1) // 2
    return lax.conv_general_dilated(
        x, w[:, None, :], window_strides=(1,), padding=[(pad, pad)],
        dimension_numbers=("NWC", "WIO", "NWC"), feature_group_count=channels)


def gated_delta_rule(q, k, v, g, beta):
    bsz, heads, seq, dk = q.shape
    dv = v.shape[-1]
    n_chunks, c = seq // DN_CHUNK, DN_CHUNK
    q = q.reshape(bsz, heads, n_chunks, c, dk)
    k = k.reshape(bsz, heads, n_chunks, c, dk)
    v = v.reshape(bsz, heads, n_chunks, c, dv)
    g = jnp.cumsum(g.reshape(bsz, heads, n_chunks, c), axis=-1)
    beta = beta.reshape(bsz, heads, n_chunks, c)
    pos = jnp.arange(c)
    incl = pos[:, None] >= pos[None, :]
    strict = pos[:, None] > pos[None, :]
    decay = jnp.exp(jnp.where(incl, g[..., :, None] - g[..., None, :], -jnp.inf))
    k_beta = k * beta[..., None]
    lower = jnp.where(strict, jnp.einsum('bhnid,bhnjd->bhnij', k_beta, k) * decay, 0.0)
    rhs = jnp.concatenate([v * beta[..., None], k_beta * jnp.exp(g)[..., None]], axis=-1)
    sol = lax.linalg.triangular_solve(lower, rhs, left_side=True, lower=True, unit_diagonal=True)
    u_c, w_c = sol[..., :dv], sol[..., dv:]
    attn = jnp.einsum('bhnid,bhnjd->bhnij', q, k) * decay
    q_dec = q * jnp.exp(g)[..., None]
    g_last = g[..., -1]
    k_tail = k * jnp.exp(g_last[..., None] - g)[..., None]
    xs = (jnp.moveaxis(q_dec, 2, 0), jnp.moveaxis(k_tail, 2, 0), jnp.moveaxis(u_c, 2, 0),
          jnp.moveaxis(w_c, 2, 0), jnp.moveaxis(attn, 2, 0), jnp.moveaxis(g_last, 2, 0))

    def step(state, inp):
        q_i, k_i, u_i, w_i, a_i, gl_i = inp
        v_new = u_i - jnp.einsum('bhcd,bhde->bhce', w_i, state)
        o_i = jnp.einsum('bhcd,bhde->bhce', q_i, state) + jnp.einsum('bhij,bhje->bhie', a_i, v_new)
        state = state * jnp.exp(gl_i)[..., None, None] + jnp.einsum('bhcd,bhce->bhde', k_i, v_new)
        return state, o_i

    s0 = jnp.zeros((bsz, heads, dk, dv), jnp.float32)
    _, o = lax.scan(step, s0, xs)
    return jnp.moveaxis(o, 0, 2).reshape(bsz, heads, seq, dv)


def deltanet_group(p, conv_w, a_log, dt_bias, norm_g):
    bsz, seq, _ = p.shape
    qkv = jax.nn.silu(centred_depthwise_conv(p[..., :OFF_Z], conv_w))
    z = p[..., OFF_Z:OFF_A]
    a = p[..., OFF_A:OFF_B].reshape(bsz, seq, 2, DN_HEADS).astype(jnp.float32)
    b = p[..., OFF_B:OFF_SG].reshape(bsz, seq, 2, DN_HEADS).astype(jnp.float32)

    def heads(t):
        return t.reshape(bsz, seq, DN_HEADS, DN_HEAD_DIM).transpose(0, 2, 1, 3).astype(jnp.float32)

    q = l2_normalize(heads(qkv[..., :DN_WIDTH])) * (DN_HEAD_DIM ** -0.5)
    k = l2_normalize(heads(qkv[..., DN_WIDTH:2 * DN_WIDTH]))
    v = heads(qkv[..., 2 * DN_WIDTH:])
    g = -jnp.exp(a_log.astype(jnp.float32)) * jax.nn.softplus(a + dt_bias.astype(jnp.float32))
    g = jnp.transpose(g, (2, 0, 3, 1))
    beta = jnp.transpose(jax.nn.sigmoid(b), (2, 0, 3, 1))
    o_fwd = gated_delta_rule(q, k, v, g[0], beta[0])
    o_bwd = jnp.flip(gated_delta_rule(jnp.flip(q, 2), jnp.flip(k, 2), jnp.flip(v, 2),
                                      jnp.flip(g[1], 2), jnp.flip(beta[1], 2)), 2)
    o = (o_fwd + o_bwd).transpose(0, 2, 1, 3)
    zf = z.reshape(bsz, seq, DN_HEADS, DN_HEAD_DIM).astype(jnp.float32)
    o = rms_norm(o, norm_g) * jax.nn.silu(zf)
    return o.reshape(bsz, seq, DN_WIDTH)


def spatial_gating_group(p, ln_g, ln_b, w_s, b_s, out_g):
    bsz, seq, _ = p.shape
    p = jax.nn.gelu(p)
    u, v = p[..., :SG_WIDTH], p[..., SG_WIDTH:]
    v = layer_norm(v, ln_g, ln_b)
    shape5 = (bsz, seq // SG_CHUNK, SG_CHUNK, SG_GROUPS, SG_GROUP_DIM)
    v = v.reshape(shape5)
    mixed = jnp.einsum('gij,bnjgc->bnigc', w_s, v) + b_s.T[:, :, None]
    y = u.reshape(shape5) * mixed
    y = rms_norm(y, out_g.reshape(SG_GROUPS, SG_GROUP_DIM))
    return y.reshape(bsz, seq, SG_WIDTH)


def setup_inputs(seed: int = 0) -> dict:
    key = jax.random.key(seed)
    ks = jax.random.split(key, 18)
    f32 = jnp.float32

    def normal(k, shape, scale):
        return jax.random.normal(k, shape, f32) * scale

    def gain(k, shape):
        return 1.0 + 0.05 * jax.random.normal(k, shape, f32)

    x = jax.random.normal(ks[0], (BATCH, SEQ, D_MODEL), f32)
    mix_norm_g = gain(ks[1], (DEPTH, D_MODEL))
    w_in = normal(ks[2], (DEPTH, D_MODEL, PROJ_WIDTH), D_MODEL ** -0.5)
    conv_w = normal(ks[3], (DEPTH, CONV_WIDTH, 3 * DN_WIDTH), CONV_WIDTH ** -0.5)
    dn_a_log = jnp.log(jax.random.uniform(ks[4], (DEPTH, 2, DN_HEADS), f32, 1.0, 16.0))
    dt = jnp.exp(jax.random.uniform(ks[5], (DEPTH, 2, DN_HEADS), f32,
                                    math.log(DT_MIN), math.log(DT_MAX)))
    dn_dt_bias = dt + jnp.log(-jnp.expm1(-dt))
    dn_norm_g = gain(ks[6], (DEPTH, DN_HEAD_DIM))
    sg_ln_g = gain(ks[7], (DEPTH, SG_WIDTH))
    sg_ln_b = normal(ks[8], (DEPTH, SG_WIDTH), 0.02)
    sg_w = normal(ks[9], (DEPTH, SG_GROUPS, SG_CHUNK, SG_CHUNK), SG_CHUNK ** -0.5)
    sg_b = gain(ks[10], (DEPTH, SG_GROUPS, SG_CHUNK))
    sg_out_g = gain(ks[11], (DEPTH, SG_WIDTH))
    w_out = normal(ks[12], (DEPTH, MIX_WIDTH, D_MODEL), MIX_WIDTH ** -0.5)
    ffn_norm_g = gain(ks[13], (DEPTH, D_MODEL))
    w_gate = normal(ks[14], (DEPTH, D_MODEL, FFN_HIDDEN), D_MODEL ** -0.5)
    w_up = normal(ks[15], (DEPTH, D_MODEL, FFN_HIDDEN), D_MODEL ** -0.5)
    w_down = normal(ks[16], (DEPTH, FFN_HIDDEN, D_MODEL), FFN_HIDDEN ** -0.5)
    final_norm_g = gain(ks[17], (D_MODEL,))
    return {"x": x, "mix_norm_g": mix_norm_g, "w_in": w_in, "conv_w": conv_w,
            "dn_a_log": dn_a_log, "dn_dt_bias": dn_dt_bias, "dn_norm_g": dn_norm_g,
            "sg_ln_g": sg_ln_g, "sg_ln_b": sg_ln_b, "sg_w": sg_w, "sg_b": sg_b,
            "sg_out_g": sg_out_g, "w_out": w_out, "ffn_norm_g": ffn_norm_g,
            "w_gate": w_gate, "w_up": w_up, "w_down": w_down, "final_norm_g": final_norm_g}


def reference(x, mix_norm_g, w_in, conv_w, dn_a_log, dn_dt_bias, dn_norm_g, sg_ln_g, sg_ln_b,
              sg_w, sg_b, sg_out_g, w_out, ffn_norm_g, w_gate, w_up, w_down, final_norm_g):
    for l in range(DEPTH):
        h = rms_norm(x, mix_norm_g[l])
        proj = jnp.einsum('bsd,dp->bsp', h, w_in[l])
        y_a = deltanet_group(proj[..., :OFF_SG], conv_w[l], dn_a_log[l], dn_dt_bias[l], dn_norm_g[l])
        y_b = spatial_gating_group(proj[..., OFF_SG:], sg_ln_g[l], sg_ln_b[l], sg_w[l], sg_b[l],
                                   sg_out_g[l])
        mix = jnp.concatenate([y_a.astype(x.dtype), y_b.astype(x.dtype)], axis=-1)
        x = x + jnp.einsum('bsm,md->bsd', mix, w_out[l])
        h = rms_norm(x, ffn_norm_g[l])
        hid = jax.nn.silu(jnp.einsum('bsd,df->bsf', h, w_gate[l])) * jnp.einsum('bsd,df->bsf', h, w_up[l])
        x = x + jnp.einsum('bsf,fd->bsd', hid, w_down[l])
    return rms_norm(x, final_norm_g)
```

```python
import math
from contextlib import ExitStack

import numpy as np

import concourse.bass as bass
import concourse.mybir as mybir
from concourse.bass_utils import run_bass_kernel_spmd

F32 = mybir.dt.float32
BF16 = mybir.dt.bfloat16
AF = mybir.ActivationFunctionType
ALU = mybir.AluOpType

D = 1024
KD = 8
PW = 3088
FF = 2816
KF = 22
TT = 512
EPS = 1e-6
C_Q, C_K, C_V, C_Z, C_AB, C_U, C_VS = 0, 512, 1024, 1536, 2048, 2064, 2576
QSCALE = 128.0 ** -0.5
GELU_C = 1.5957691216057308

PP_GMIX, PP_GFFN, PP_CONV, PP_DNNG, PP_SGOG, PP_N = 0, 8, 16, 52, 53, 57
BP_ALOG, BP_OFF, BP_SGN, BP_LNG, BP_LNB, BP_BSB, BP_N = 0, 16, 32, 48, 560, 1072, 1584
CS_ID, CS_MC0, CS_MC1, CS_BD, CS_NM0, CS_NM1, CS_ST0, CS_ST1, CS_N = 0, 128, 256, 384, 512, 640, 768, 896, 1024


class _I:
    __slots__ = ("fn", "deps", "dma", "dkey", "val", "needs_inc")

    def __init__(self, fn, deps, dma, dkey):
        self.fn = fn
        self.deps = deps
        self.dma = dma
        self.dkey = dkey
        self.val = 0
        self.needs_inc = False


class Prog:
    ENGS = ("sync", "act", "dve", "pool", "pe")

    def __init__(self, nc):
        self.nc = nc
        self.ins = {e: [] for e in self.ENGS}
        self.last_w = {}
        self.readers = {}
        self.dcount = {}

    def op(self, eng, fn, reads=(), writes=(), dkey=None):
        deps = set()
        for k in reads:
            lw = self.last_w.get(k)
            if lw is not None:
                deps.add(lw)
        for k in writes:
            lw = self.last_w.get(k)
            if lw is not None:
                deps.add(lw)
            for r in self.readers.get(k, ()):
                deps.add(r)
        idx = len(self.ins[eng])
        dma = dkey is not None
        if eng == "pe" and not dma:
            deps = {d for d in deps if d[0] != "pe"}
        it = _I(fn, deps, dma, dkey)
        if dma:
            c = self.dcount.get(dkey, 0) + 16
            self.dcount[dkey] = c
            it.val = c
        self.ins[eng].append(it)
        me = (eng, idx)
        for k in reads:
            lst = self.readers.setdefault(k, [])
            if not dma:
                lst[:] = [r for r in lst if not (r[0] == eng and not self.ins[r[0]][r[1]].dma)]
            lst.append(me)
        for k in writes:
            self.last_w[k] = me
            self.readers[k] = []
        return me

    def emit(self, stack):
        nc = self.nc
        for e in self.ENGS:
            for it in self.ins[e]:
                for (de, di) in it.deps:
                    self.ins[de][di].needs_inc = True
        csem = {}
        for e in self.ENGS:
            csem[e] = stack.enter_context(nc.semaphore("c_" + e))
            n = 0
            for it in self.ins[e]:
                if not it.dma and it.needs_inc:
                    n += 1
                    it.val = n
        dsem = {}
        for k in self.dcount:
            dsem[k] = stack.enter_context(nc.semaphore("d_" + str(k)))
        block = stack.enter_context(nc.Block())
        ins = self.ins

        def run(e, h):
            waited = {}
            for it in ins[e]:
                need = {}
                for (de, di) in it.deps:
                    d = ins[de][di]
                    s = dsem[d.dkey] if d.dma else csem[de]
                    key = id(s)
                    if waited.get(key, 0) >= d.val:
                        continue
                    if key not in need or need[key][1] < d.val:
                        need[key] = (s, d.val)
                for key, (s, v) in need.items():
                    h.wait_ge(s, v)
                    waited[key] = v
                r = it.fn(h)
                if it.dma:
                    r.then_inc(dsem[it.dkey], 16)
                elif it.needs_inc:
                    r.then_inc(csem[e], 1)

        @block.sync
        def _(h):
            run("sync", h)

        @block.scalar
        def _(h):
            run("act", h)

        @block.vector
        def _(h):
            run("dve", h)

        @block.gpsimd
        def _(h):
            run("pool", h)

        @block.tensor
        def _(h):
            run("pe", h)


def build(NT, L, dbg=None):
    NTT = NT // TT
    dbg = dbg or set()
    nc = bass.Bass("TRN2", target_bir_lowering=False)

    def din(name, shape, dt=F32):
        return nc.dram_tensor(name, list(shape), dt, kind="ExternalInput").ap()

    xin = din("xin", [128, KD, NT + 1])
    w_in = din("w_in", [L, D, PW])
    w_out = din("w_out", [L, D, D])
    w_gate = din("w_gate", [L, D, FF])
    w_up = din("w_up", [L, D, FF])
    w_down = din("w_down", [L, FF, D])
    pp_d = din("pp", [128, L, PP_N])
    fg_d = din("fg", [128, KD])
    bp_d = din("bp", [128, L, BP_N])
    sgw_d = din("sgwT", [L, 128, 512])
    cs_d = din("consts", [128, CS_N])
    out_d = nc.dram_tensor("out", [128, KD, NT], F32, kind="ExternalOutput").ap()

    xs1 = nc.dram_tensor("xs1", [128, KD, NT + 1], F32, kind="Internal").ap()
    scr_qkv = nc.dram_tensor("scr_qkv", [NTT, 128, 12 * TT], BF16, kind="Internal").ap()
    scr_sz = nc.dram_tensor("scr_sz", [NTT, 128, 4 * TT], BF16, kind="Internal").ap()
    scr_ysg = nc.dram_tensor("scr_ysg", [NTT, 128, 4 * TT], BF16, kind="Internal").ap()
    scr_o1 = nc.dram_tensor("scr_o1", [NTT, 128, 4 * TT], BF16, kind="Internal").ap()
    scr_ab = nc.dram_tensor("scr_ab", [NTT, 128, 64], F32, kind="Internal").ap()

    dbg_outs = {}

    with ExitStack() as st:
        def sb(name, shape, dt=F32):
            return st.enter_context(nc.sbuf_tensor("s_" + name, list(shape), dt))

        def pst(name, shape, dt=F32):
            return st.enter_context(nc.psum_tensor("p_" + name, list(shape), dt))

        cs = sb("cs", [128, CS_N])
        ident_b = sb("ident_b", [128, 128], BF16)
        ones_b = sb("ones_b", [128, 128], BF16)
        ones_f = sb("ones_f", [128, 128])
        negm_b = sb("negm_b", [128, 2, 512], BF16)
        st01 = sb("st01", [128, 2, 512])
        zero_f = sb("zero_f", [128, 8])
        pp = sb("pp", [128, PP_N])
        g32 = sb("g32", [128, 16])
        fg = sb("fg", [128, KD])
        bp = sb("bp", [128, BP_N])
        mul16 = sb("mul16", [128, 16])
        wsT = sb("wsT", [128, 512], BF16)
        wab = sb("wab", [128, KD, 16], BF16)
        xt = sb("xt", [128, KD, TT + 1])
        hT = sb("hT", [128, KD, TT + 1], BF16)
        sqs = sb("sqs", [128, 2, TT + 1], BF16)
        rstd_b = sb("rstd_b", [128, TT + 1])
        rstd_tm = sb("rstd_tm", [128, 4])
        tmp_s = sb("tmp_s", [128, TT + 8])
        NWS = 4
        wsl = [sb("wsl%d" % i, [128, 4096], BF16) for i in range(NWS)]
        arena = sb("arena", [128, 12 * (TT + 2)])
        pbuf = arena[:, :].rearrange("p (c t) -> p c t", c=12)
        hidT = arena[:, 0:KF * TT // 2].bitcast(BF16).rearrange("p (k t) -> p k t", k=KF)
        dummy = sb("dummy", [128, 2])
        carry = sb("carry", [128, 12])
        cv = sb("cv", [128, 2, TT])
        uT = sb("uT", [128, 4, TT])
        gt = sb("gt", [128, 3, TT])
        vln = sb("vln", [128, 4, 512], BF16)
        qkvT2 = [sb("qkvT_%d" % i, [128, 12, TT], BF16) for i in range(2)]
        szT = sb("szT", [128, 4, TT], BF16)
        mixT = sb("mixT", [128, 8, TT], BF16)
        o1T = sb("o1T", [128, 4, TT], BF16)
        oacc = sb("oacc", [128, 4, TT])
        ab_tm2 = sb("ab_tm", [128, 2, 4, 16])
        stats = sb("stats", [128, 8])
        stats2 = sb("stats2", [128, 2])
        sqh = sb("sqh", [128, 8], BF16)
        t16 = sb("t16", [128, 4, 16])
        G16 = sb("G16", [128, 4, 16])
        sc12 = sb("sc12", [128, 12])
        esc = sb("esc", [128, 12])
        ngc = sb("ngc", [128, 4])
        nbe = sb("nbe", [128, 4])
        rhsP = sb("rhsP", [128, 4, 128])
        decT = sb("decT", [128, 4, 128])
        EGb = sb("EGb", [128, 4, 128])
        Ebs = sb("Ebs", [128, 4, 128])
        qgT = sb("qgT", [128, 4, 128], BF16)
        kg = sb("kg", [128, 4, 128], BF16)
        ktl = sb("ktl", [128, 4, 128], BF16)
        vtm = sb("vtm", [128, 4, 128], BF16)
        Nm = sb("Nm", [128, 4, 128], BF16)
        NmT = sb("NmT", [128, 4, 128], BF16)
        Xa = [sb("Xa%d" % i, [128, 4, 128], BF16) for i in range(2)]
        XTa = [sb("XTa%d" % i, [128, 4, 128], BF16) for i in range(2)]
        Pa = [sb("Pa%d" % i, [128, 4, 128], BF16) for i in range(2)]
        attT = sb("attT", [128, 4, 128], BF16)
        ub = sb("ub", [128, 4, 128])
        wT = sb("wT", [128, 4, 128], BF16)
        vnew = sb("vnew", [128, 4, 128], BF16)
        S_f = sb("S_f", [128, 4, 128])
        S_b = sb("S_b", [128, 4, 128], BF16)

        psA = pst("psA", [128, 512])
        psB = pst("psB", [128, 512])
        psQ = pst("psQ", [128, 512])
        psD = pst("psD", [128, 512])
        psE = pst("psE", [128, 1024])
        psG = pst("psG", [128, 512])
        psTS = pst("psTS", [128, 512])
        psT = psTS[:, 0:256].bitcast(BF16).rearrange("p (h i) -> p h i", h=4)
        psS = psTS[:, 384:512]

        P = Prog(nc)

        def mm(out, lhsT, rhs, start, stop, r, w):
            P.op("pe", lambda h: h.matmul(out, lhsT=lhsT, rhs=rhs, start=start, stop=stop), r, w)

        def tr(out, in_, r, w):
            P.op("pe", lambda h: h.transpose(out, in_, ident_b[:]), list(r) + ["ident_b"], w)

        def tt_(eng, out, in0, in1, op, r, w):
            P.op(eng, lambda h: h.tensor_tensor(out=out, in0=in0, in1=in1, op=op), r, w)

        def ts_(eng, out, in0, s1, s2, op0, op1, r, w):
            if s2 is None:
                P.op(eng, lambda h: h.tensor_scalar(out=out, in0=in0, scalar1=s1, scalar2=None, op0=op0), r, w)
            else:
                P.op(eng, lambda h: h.tensor_scalar(out=out, in0=in0, scalar1=s1, scalar2=s2, op0=op0, op1=op1), r, w)

        def stt(eng, out, in0, scalar, in1, op0, op1, r, w):
            P.op(eng, lambda h: h.scalar_tensor_tensor(out=out, in0=in0, scalar=scalar, in1=in1, op0=op0, op1=op1), r, w)

        def act(out, in_, func, r, w, bias=0.0, scale=1.0):
            P.op("act", lambda h: h.activation(out=out, in_=in_, func=func, bias=bias, scale=scale), r, w)

        def cp(eng, out, in_, r, w):
            if eng == "act":
                P.op("act", lambda h: h.copy(out=out, in_=in_), r, w)
            else:
                P.op(eng, lambda h: h.tensor_copy(out=out, in_=in_), r, w)

        def recip(out, in_, r, w):
            P.op("dve", lambda h: h.reciprocal(out=out, in_=in_), r, w)

        def dma(eng, out, in_, r, w, dkey, slow=False):
            if slow:
                P.op(eng, lambda h: h.dma_start(out=out, in_=in_, allow_slow_non_contiguous=True), r, w, dkey=dkey)
            else:
                P.op(eng, lambda h: h.dma_start(out=out, in_=in_), r, w, dkey=dkey)

        def memset(eng, ap, val, w):
            P.op(eng, lambda h: h.memset(ap, val), (), w)

        def dump(name, ap, shape, dt, rkeys):
            if name not in dbg:
                return
            o = nc.dram_tensor("dbg_" + name, list(shape), dt, kind="ExternalOutput").ap()
            dbg_outs[name] = o
            dma("sync", o, ap, rkeys, ["dbg_" + name], "dbg_" + name)

        def rsqrt(out, in_, r, w, bias, scale=1.0):
            act(out, in_, AF.Ln, r, w, bias=bias, scale=scale)
            act(out, out, AF.Exp, w, w, scale=-0.5)

        plan = []
        for l in range(L):
            for tt in range(NTT):
                for nm, c0 in (("q", C_Q), ("k", C_K), ("v", C_V), ("z", C_Z), ("u", C_U), ("vs", C_VS)):
                    plan.append(("in", l, c0, 512))
            for tt in range(NTT):
                for g in range(2):
                    plan.append(("out", l, g * 512, 512))
                for g in range(6):
                    n = 512 if g < 5 else 256
                    plan.append(("gate", l, g * 512, n))
                    plan.append(("up", l, g * 512, n))
                for m in range(8):
                    plan.append(("down", l, m * 128, 128))
        wstate = {"issued": 0, "used": 0}

        def w_issue(i):
            kind, l, c0, n = plan[i]
            slot = i % NWS
            key = "wsl%d" % slot
            if kind == "down":
                src = w_down[l].rearrange("(k p) c -> p k c", p=128)[:, :, c0:c0 + n]
                dst = wsl[slot][:, 0:KF * n].rearrange("p (k c) -> p k c", k=KF)
            else:
                wsrc = {"in": w_in, "out": w_out, "gate": w_gate, "up": w_up}[kind]
                src = wsrc[l].rearrange("(k p) c -> p k c", p=128)[:, :, c0:c0 + n]
                dst = wsl[slot][:, 0:KD * n].rearrange("p (k c) -> p k c", k=KD)
            dma("pool", dst, src, [], [key], key)

        def w_next(kind, l, c0):
            i = wstate["used"]
            assert plan[i][0] == kind and plan[i][1] == l and plan[i][2] == c0, (plan[i], kind, l, c0)
            while wstate["issued"] < min(len(plan), i + NWS - 1):
                w_issue(wstate["issued"])
                wstate["issued"] += 1
            wstate["used"] += 1
            slot = i % NWS
            n = plan[i][3]
            k = KF if kind == "down" else KD
            return wsl[slot][:, 0:k * n].rearrange("p (k c) -> p k c", k=k), "wsl%d" % slot

        dma("sync", cs[:], cs_d[:], [], ["cs"], "cs")
        cp("dve", ident_b[:], cs[:, CS_ID:CS_ID + 128], ["cs"], ["ident_b"])
        memset("dve", ones_b[:], 1.0, ["ones_b"])
        memset("dve", ones_f[:], 1.0, ["ones_f"])
        memset("dve", zero_f[:], 0.0, ["zero_f"])
        for r_ in range(2):
            c_nm = CS_NM0 if r_ == 0 else CS_NM1
            c_st = CS_ST0 if r_ == 0 else CS_ST1
            cp("dve", negm_b[:, r_, :].rearrange("p (h i) -> p h i", h=4),
               cs[:, c_nm:c_nm + 128].unsqueeze(1).to_broadcast([128, 4, 128]), ["cs"], ["negm_b"])
            cp("dve", st01[:, r_, :].rearrange("p (h i) -> p h i", h=4),
               cs[:, c_st:c_st + 128].unsqueeze(1).to_broadcast([128, 4, 128]), ["cs"], ["st01"])
        dma("sync", fg[:], fg_d[:], [], ["fg"], "fg")
        ts_("dve", fg[:], fg[:], 32.0, None, ALU.mult, None, ["fg"], ["fg"])
        dma("sync", xs1[:, :, NT:NT + 1], zero_f[:, :].unsqueeze(2), ["zero_f"], ["xs1_%d" % NTT], "xs1h", slow=True)
        Mc = [cs[:, CS_MC0:CS_MC0 + 128], cs[:, CS_MC1:CS_MC1 + 128]]
        BD = cs[:, CS_BD:CS_BD + 128]

        def load_layer_params(l):
            dma("sync", pp[:], pp_d[:, l, :], [], ["pp"], "pp")
            ts_("dve", g32[:], pp[:, 0:16], 32.0, None, ALU.mult, None, ["pp"], ["g32"])
            dma("sync", bp[:], bp_d[:, l, :], [], ["bp"], "bp")
            act(mul16[:], bp[:, BP_ALOG:BP_ALOG + 16], AF.Exp, ["bp"], ["mul16"])
            ts_("dve", mul16[:], mul16[:], -1.0, None, ALU.mult, None, ["mul16"], ["mul16"])
            dma("pool", wsT[:], sgw_d[l], [], ["wsT"], "wsT")
            dma("pool", wab[:], w_in[l].rearrange("(k p) c -> p k c", p=128)[:, :, C_AB:C_AB + 16], [], ["wab"], "wab")

        def load_x(src, tt, halo):
            t0 = tt * TT
            n = TT + 1 if halo else TT
            rk = ["%s_%d" % (src[1], tt)] + (["%s_%d" % (src[1], tt + 1)] if halo else [])
            dma("sync", xt[:, :, 0:n], src[0][:, :, t0:t0 + n], rk, ["xt"], "xt")

        dense_ctr = [0]

        def dense_ps():
            i = dense_ctr[0]
            dense_ctr[0] += 1
            return (psA, "psA") if i % 2 == 0 else (psB, "psB")

        def norm_stage(gcol, halo, tokmajor):
            n = TT + 1 if halo else TT
            for k in range(KD):
                s = k % 2
                act(sqs[:, s, 0:TT], xt[:, k, 0:TT], AF.Square, ["xt"], ["sqs%d" % s])
                mm(psQ[:, 0:TT], ones_b[:], sqs[:, s, 0:TT], k == 0, k == KD - 1, ["ones_b", "sqs%d" % s], ["psQ"])
            rsqrt(rstd_b[:, 0:TT], psQ[:, 0:TT], ["psQ"], ["rstd_b"], bias=1024 * EPS)
            rk = ["rstd_b"]
            if halo:
                act(sqh[:, 0:8], xt[:, :, TT], AF.Square, ["xt"], ["sqh"])
                psh, phk = dense_ps()
                mm(psh[:, 0:8], ones_b[:], sqh[:, 0:8], True, True, ["ones_b", "sqh"], [phk])
                P.op("dve", lambda h: h.reduce_sum(out=stats2[:, 0:1], in_=psh[:, 0:8], axis=mybir.AxisListType.X),
                     [phk], ["stats2"])
                rsqrt(rstd_b[:, TT:TT + 1], stats2[:, 0:1], ["stats2"], ["rstd_bh"], bias=1024 * EPS)
                rk = ["rstd_b", "rstd_bh"]
            for k in range(KD):
                ts_("pool", tmp_s[:, 0:n], xt[:, k, 0:n], g32[:, gcol + k:gcol + k + 1], None, ALU.mult, None, ["xt", "g32"], ["tmp_s"])
                tt_("pool", hT[:, k, 0:n], tmp_s[:, 0:n], rstd_b[:, 0:n], ALU.mult, ["tmp_s"] + rk, ["hT"])

        def gelu(x_ap, out_ap, xk, ok, tslot):
            a = gt[:, tslot, :]
            ak = "gt%d" % tslot
            tt_("pool", a, x_ap, x_ap, ALU.mult, xk, [ak])
            ts_("pool", a, a, 0.044715, 1.0, ALU.mult, ALU.add, [ak], [ak])
            tt_("pool", a, a, x_ap, ALU.mult, [ak] + list(xk), [ak])
            act(a, a, AF.Sigmoid, [ak], [ak], scale=GELU_C)
            tt_("pool", out_ap, x_ap, a, ALU.mult, [ak] + list(xk), ok)

        def qkeys(qb):
            return ["qkvT%d_%d" % (qb, j) for j in range(12)]

        def mk_step(gen, ratio):
            acc = [0.0]

            def step():
                if gen is None:
                    return
                acc[0] += ratio
                while acc[0] >= 1.0:
                    acc[0] -= 1.0
                    try:
                        next(gen)
                    except StopIteration:
                        return
            return step

        def proj_tile(l, tt, xsrc, gen):
            step = mk_step(gen, 2.3)
            first = tt == 0
            qb = tt % 2
            qkvT = qkvT2[qb]
            ab_tm = ab_tm2[:, qb]
            load_x(xsrc, tt, True)
            norm_stage(PP_GMIX, True, True)
            if first:
                dump("hT", hT[:], [128, KD, TT + 1], BF16, ["hT"])
                dump("rstd_b", rstd_b[:], [128, TT + 1], F32, ["rstd_b", "rstd_bh"])
                memset("pool", carry[:], 0.0, ["carry0", "carry1", "carry2"])
            step()
            for gi, c0 in enumerate((C_Q, C_K, C_V)):
                wv, wk = w_next("in", l, c0)
                for mi in range(4):
                    j = gi * 4 + mi
                    ps, pk = dense_ps()
                    for k in range(KD):
                        mm(ps[:, 0:TT], wv[:, k, mi * 128:(mi + 1) * 128], hT[:, k, 0:TT], k == 0, k == KD - 1, [wk, "hT"], [pk])
                    cp("act", pbuf[:, j, 1:TT + 1], ps[:, 0:TT], [pk], ["pbuf%d" % j])
                psh, phk = dense_ps()
                for mi in range(4):
                    for k in range(KD):
                        mm(psh[:, mi:mi + 1], wv[:, k, mi * 128:(mi + 1) * 128], hT[:, k, TT:TT + 1], k == 0, k == KD - 1,
                           [wk, "hT"], [phk])
                jk = ["pbuf%d" % j for j in range(gi * 4, gi * 4 + 4)]
                cp("pool", pbuf[:, gi * 4:gi * 4 + 4, 0], carry[:, gi * 4:gi * 4 + 4], ["carry%d" % gi], jk)
                cp("act", pbuf[:, gi * 4:gi * 4 + 4, TT + 1], psh[:, 0:4], [phk], jk)
                cp("pool", carry[:, gi * 4:gi * 4 + 4], pbuf[:, gi * 4:gi * 4 + 4, TT], jk, ["carry%d" % gi])
                for mi in range(4):
                    j = gi * 4 + mi
                    s = j % 2
                    ck = "cv%d" % s
                    pk_ = "pbuf%d" % j
                    act(cv[:, s, :], pbuf[:, j, 1:TT + 1], AF.Copy, [pk_, "pp"], [ck], scale=pp[:, PP_CONV + 3 * j + 1:PP_CONV + 3 * j + 2])
                    gk = "gt%d" % s
                    ts_("pool", gt[:, s, :], pbuf[:, j, 0:TT], pp[:, PP_CONV + 3 * j:PP_CONV + 3 * j + 1], None, ALU.mult, None, [pk_, "pp"], [gk])
                    tt_("pool", cv[:, s, :], cv[:, s, :], gt[:, s, :], ALU.add, [ck, gk], [ck])
                    ts_("pool", gt[:, s, :], pbuf[:, j, 2:TT + 2], pp[:, PP_CONV + 3 * j + 2:PP_CONV + 3 * j + 3], None, ALU.mult, None, [pk_, "pp"], [gk])
                    tt_("pool", cv[:, s, :], cv[:, s, :], gt[:, s, :], ALU.add, [ck, gk], [ck])
                    if j >= 8:
                        act(qkvT[:, j, :], cv[:, s, :], AF.Silu, [ck], ["qkvT%d_%d" % (qb, j)])
                    else:
                        act(cv[:, s, :], cv[:, s, :], AF.Silu, [ck], [ck])
                        act(sqs[:, s, 0:TT], cv[:, s, :], AF.Square, [ck], ["sqs%d" % s])
                        mm(psQ[:, 0:TT], ones_b[:], sqs[:, s, 0:TT], True, True, ["ones_b", "sqs%d" % s], ["psQ"])
                        rsqrt(tmp_s[:, 0:TT], psQ[:, 0:TT], ["psQ"], ["tmp_s"], bias=EPS)
                        tt_("pool", qkvT[:, j, :], cv[:, s, :], tmp_s[:, 0:TT], ALU.mult, [ck, "tmp_s"], ["qkvT%d_%d" % (qb, j)])
                    step()
            if first:
                dump("pbuf", pbuf[:, :, :], [128, 12, TT + 2], F32, ["pbuf%d" % j for j in range(12)])
                dump("qkvT", qkvT[:], [128, 12, TT], BF16, qkeys(qb))
            dma("sync", scr_qkv[tt].rearrange("p (c t) -> p c t", c=12), qkvT[:], qkeys(qb), ["scr_qkv%d" % tt], "st_qkv%d" % qb)
            wv, wk = w_next("in", l, C_Z)
            for mi in range(4):
                ps, pk = dense_ps()
                for k in range(KD):
                    mm(ps[:, 0:TT], wv[:, k, mi * 128:(mi + 1) * 128], hT[:, k, 0:TT], k == 0, k == KD - 1, [wk, "hT"], [pk])
                act(szT[:, mi, :], ps[:, 0:TT], AF.Silu, [pk], ["szT"])
                step()
            dma("sync", scr_sz[tt].rearrange("p (c t) -> p c t", c=4), szT[:], ["szT"], ["scr_sz%d" % tt], "st_sz")
            wv, wk = w_next("in", l, C_U)
            for mi in range(4):
                ps, pk = dense_ps()
                for k in range(KD):
                    mm(ps[:, 0:TT], wv[:, k, mi * 128:(mi + 1) * 128], hT[:, k, 0:TT], k == 0, k == KD - 1, [wk, "hT"], [pk])
                cp("act", gt[:, 1, :], ps[:, 0:TT], [pk], ["gt1"])
                gelu(gt[:, 1, :], uT[:, mi, :], ["gt1"], ["uT%d" % mi], 0)
                step()
            psh, phk = dense_ps()
            for b in range(4):
                for k in range(KD):
                    mm(psh[:, 16 * b:16 * b + 16], hT[:, k, b * 128:(b + 1) * 128], wab[:, k, :], k == 0, k == KD - 1,
                       ["wab", "hT"], [phk])
            cp("act", ab_tm, psh[:, 0:64].rearrange("p (b c) -> p b c", b=4), [phk], ["ab_tm%d" % qb])
            dma("sync", scr_ab[tt].rearrange("p (b c) -> p b c", b=4), ab_tm, ["ab_tm%d" % qb], ["scr_ab%d" % tt], "st_ab%d" % qb)
            if first:
                dump("ab_tm", ab_tm, [128, 4, 16], F32, ["ab_tm%d" % qb])
            wv, wk = w_next("in", l, C_VS)
            for b in range(4):
                ps, pk = dense_ps()
                for k in range(KD):
                    mm(ps[:, 0:512], hT[:, k, b * 128:(b + 1) * 128], wv[:, k, :], k == 0, k == KD - 1, [wk, "hT"], [pk])
                cp("act", gt[:, 1, :], ps[:, 0:512], [pk], ["gt1"])
                gelu(gt[:, 1, :], gt[:, 2, :], ["gt1"], ["gt2"], 0)
                P.op("dve", lambda h: h.bn_stats(out=stats[:, 0:6], in_=gt[:, 2, :]), ["gt2"], ["stats"])
                P.op("dve", lambda h: h.bn_aggr(out=stats[:, 6:8], in_=stats[:, 0:6]), ["stats"], ["stats"])
                rsqrt(stats[:, 7:8], stats[:, 7:8], ["stats"], ["stats"], bias=EPS)
                ts_("pool", gt[:, 2, :], gt[:, 2, :], stats[:, 6:7], stats[:, 7:8], ALU.subtract, ALU.mult, ["gt2", "stats"], ["gt2"])
                tt_("pool", gt[:, 2, :], gt[:, 2, :], bp[:, BP_LNG:BP_LNG + 512], ALU.mult, ["gt2", "bp"], ["gt2"])
                tt_("pool", vln[:, b, :], gt[:, 2, :], bp[:, BP_LNB:BP_LNB + 512], ALU.add, ["gt2", "bp"], ["vln%d" % b])
                step()
            if first:
                dump("vln", vln[:], [128, 4, 512], BF16, ["vln%d" % b for b in range(4)])
            for g in range(4):
                ps, pk = dense_ps()
                for b in range(4):
                    mm(ps[:, b * 128:(b + 1) * 128], vln[:, b, g * 128:(g + 1) * 128], wsT[:, g * 128:(g + 1) * 128], True, True,
                       ["vln%d" % b, "wsT"], [pk])
                y = gt[:, 1, :]
                cp("act", y, ps[:, 0:512], [pk], ["gt1"])
                tt_("pool", y.rearrange("p (b i) -> p b i", b=4), y.rearrange("p (b i) -> p b i", b=4),
                    bp[:, BP_BSB + g * 128:BP_BSB + (g + 1) * 128].unsqueeze(1).to_broadcast([128, 4, 128]), ALU.add,
                    ["gt1", "bp"], ["gt1"])
                tt_("pool", y, y, uT[:, g, :], ALU.mult, ["gt1", "uT%d" % g], ["gt1"])
                act(sqs[:, 0, 0:TT], y, AF.Square, ["gt1"], ["sqs0"])
                mm(psQ[:, 0:TT], ones_b[:], sqs[:, 0, 0:TT], True, True, ["ones_b", "sqs0"], ["psQ"])
                rsqrt(tmp_s[:, 0:TT], psQ[:, 0:TT], ["psQ"], ["tmp_s"], bias=EPS, scale=1.0 / 128)
                ts_("pool", y, y, pp[:, PP_SGOG + g:PP_SGOG + g + 1], None, ALU.mult, None, ["gt1", "pp"], ["gt1"])
                tt_("pool", mixT[:, 4 + g, :], y, tmp_s[:, 0:TT], ALU.mult, ["gt1", "tmp_s"], ["mixT_sg"])
                step()
            if first:
                dump("ysg", mixT[:, 4:8, :], [128, 4, TT], BF16, ["mixT_sg"])
            dma("sync", scr_ysg[tt].rearrange("p (c t) -> p c t", c=4), mixT[:, 4:8, :], ["mixT_sg"], ["scr_ysg%d" % tt], "st_ysg")

        def delta1_gen(tt):
            yield from delta_tile(tt, 0)
            if tt == 0:
                dump("o1T", o1T[:], [128, 4, TT], BF16, ["o1T"])
            dma("sync", scr_o1[tt].rearrange("p (c t) -> p c t", c=4), o1T[:], ["o1T"], ["scr_o1%d" % tt], "st_o1")

        PSE0 = ["psE0"]
        PSE1 = ["psE1"]
        PSE = PSE0 + PSE1

        def delta_tile(tt, r):
            qb = tt % 2
            tt_("dve", t16[:], ab_tm2[:, qb], bp[:, BP_SGN:BP_SGN + 16].unsqueeze(1).to_broadcast([128, 4, 16]), ALU.mult,
                ["ab_tm%d" % qb, "bp"], ["t16"])
            tt_("dve", t16[:], t16[:], bp[:, BP_OFF:BP_OFF + 16].unsqueeze(1).to_broadcast([128, 4, 16]), ALU.add,
                ["t16", "bp"], ["t16"])
            act(t16[:], t16[:], AF.Exp, ["t16"], ["t16"])
            act(t16[:], t16[:], AF.Ln, ["t16"], ["t16"], bias=1.0)
            tt_("dve", G16[:], t16[:], mul16[:].unsqueeze(1).to_broadcast([128, 4, 16]), ALU.mult, ["t16", "mul16"], ["G16"])
            if tt == 0 and r == 0:
                dump("G16", G16[:], [128, 4, 16], F32, ["G16"])
            yield
            blocks = range(4) if r == 0 else range(3, -1, -1)
            for b in blocks:
                yield from delta_block(tt, b, r)

        def delta_block(tt, b, r):
            qb = tt % 2
            qkvT = qkvT2[qb]
            qk_keys = qkeys(qb)
            c0 = b * 128
            g4 = G16[:, b, r * 4:r * 4 + 4]
            lb4 = G16[:, b, 8 + r * 4:12 + r * 4]
            dbg0 = (tt == 0 and b == 0 and r == 0)
            mm(psS[:, 96:100], Mc[r], g4, True, True, ["cs", "G16"], ["psTS"])
            mm(psS[:, 100:104], BD, g4, True, True, ["cs", "G16"], ["psTS"])
            cp("dve", sc12[:, 0:4], psS[:, 96:100], ["psTS"], ["sc12"])
            tt_("dve", sc12[:, 4:8], psS[:, 100:104], sc12[:, 0:4], ALU.subtract, ["psTS", "sc12"], ["sc12"])
            cp("dve", sc12[:, 8:12], lb4, ["G16", "sc12"], ["sc12"])
            act(esc[:], sc12[:], AF.Exp, ["sc12"], ["esc"])
            ts_("dve", ngc[:], sc12[:, 0:4], -1.0, None, ALU.mult, None, ["sc12"], ["ngc"])
            ts_("dve", nbe[:], esc[:, 8:12], -1.0, None, ALU.mult, None, ["esc"], ["nbe"])
            yield
            tt_("dve", rhsP[:], Mc[r].unsqueeze(1).to_broadcast([128, 4, 128]), g4.unsqueeze(2).to_broadcast([128, 4, 128]),
                ALU.mult, ["cs", "G16"], ["rhsP"])
            psD3 = psD[:, 0:512].rearrange("p (h i) -> p h i", h=4)
            mm(psD[:, 0:512], ones_f[:], rhsP[:].rearrange("p h i -> p (h i)"), True, True, ["ones_f", "rhsP"], ["psD"])
            act(EGb[:], psD3, AF.Exp, ["psD"], ["EGb"])
            psG3 = psG[:, 0:512].rearrange("p (h i) -> p h i", h=4)
            mm(psG[:, 0:512], ones_f[:], rhsP[:].rearrange("p h i -> p (h i)"), True, False, ["ones_f", "rhsP"], ["psG"])
            mm(psG[:, 0:512], ident_b[:], negm_b[:, r, :], False, True, ["ident_b", "negm_b"], ["psG"])
            for h in range(4):
                act(decT[:, h, :], psG3[:, h, :], AF.Exp, ["psG", "ngc"], ["decT"], bias=ngc[:, h:h + 1])
            if dbg0:
                dump("decT", decT[:], [128, 4, 128], F32, ["decT"])
                dump("EGb", EGb[:], [128, 4, 128], F32, ["EGb"])
            yield
            stt("dve", qgT[:], qkvT[:, 0:4, c0:c0 + 128], QSCALE, EGb[:], ALU.mult, ALU.mult, qk_keys + ["EGb"], ["qgT"])
            for h in range(4):
                tr(psT[:, h, :], qkvT[:, 4 + h, c0:c0 + 128], qk_keys, ["psTS"])
            tt_("dve", kg[:], psT[:], esc[:, 0:4].unsqueeze(2).to_broadcast([128, 4, 128]), ALU.mult, ["psTS", "esc"], ["kg"])
            tt_("dve", ktl[:], psT[:], esc[:, 4:8].unsqueeze(2).to_broadcast([128, 4, 128]), ALU.mult, ["psTS", "esc"], ["ktl"])
            for h in range(4):
                tr(psT[:, h, :], qkvT[:, 8 + h, c0:c0 + 128], qk_keys, ["psTS"])
            cp("dve", vtm[:], psT[:], ["psTS"], ["vtm"])
            yield
            psE4 = psE[:, :].rearrange("p (h two i) -> p h two i", h=4, two=2)
            for h in range(4):
                mm(psE4[:, h, 0, :], qkvT[:, 4 + h, c0:c0 + 128], qkvT[:, 4 + h, c0:c0 + 128], True, True, qk_keys, PSE)
                mm(psE4[:, h, 1, :], qkvT[:, 4 + h, c0:c0 + 128], qkvT[:, h, c0:c0 + 128], True, True, qk_keys, PSE)
            tt_("dve", Ebs[:], decT[:], esc[:, 8:12].unsqueeze(2).to_broadcast([128, 4, 128]), ALU.mult, ["decT", "esc"], ["Ebs"])
            tt_("dve", Ebs[:], Ebs[:], st01[:, r, :].rearrange("p (h i) -> p h i", h=4), ALU.mult, ["Ebs", "st01"], ["Ebs"])
            tt_("dve", Nm[:], psE4[:, :, 0, :], Ebs[:], ALU.mult, PSE + ["Ebs"], ["Nm"])
            tt_("dve", Pa[0][:], ident_b[:].unsqueeze(1).to_broadcast([128, 4, 128]), Nm[:], ALU.subtract, ["ident_b", "Nm"], ["Pa0"])
            stt("dve", attT[:], psE4[:, :, 1, :], QSCALE, decT[:], ALU.mult, ALU.mult, PSE + ["decT"], ["attT"])
            if dbg0:
                dump("Nm", Nm[:], [128, 4, 128], BF16, ["Nm"])
                dump("attT", attT[:], [128, 4, 128], BF16, ["attT"])
            yield
            for h in range(4):
                tr(psT[:, h, :], Nm[:, h, :], ["Nm"], ["psTS"])
            cp("dve", NmT[:], psT[:], ["psTS"], ["NmT"])
            X, Xk, XT, XTk = Nm, "Nm", NmT, "NmT"
            pcur = 0
            psG3 = psG[:, 0:512].rearrange("p (h i) -> p h i", h=4)
            psE0_3 = psE[:, 0:512].rearrange("p (h i) -> p h i", h=4)
            psE1_3 = psE[:, 512:1024].rearrange("p (h i) -> p h i", h=4)
            for lvl in range(5):
                s = lvl % 2
                last = lvl == 4
                for h in range(4):
                    mm(psE0_3[:, h, :], X[:, h, :], XT[:, h, :], True, True, [Xk, XTk], PSE0)
                if not last:
                    for h in range(4):
                        mm(psG3[:, h, :], XT[:, h, :], X[:, h, :], True, True, [Xk, XTk], ["psG"])
                cp("dve", XTa[s][:], psE0_3, PSE0, ["XTa%d" % s])
                if not last:
                    cp("dve", Xa[s][:], psG3, ["psG"], ["Xa%d" % s])
                yield
                X, Xk, XT, XTk = Xa[s], "Xa%d" % s, XTa[s], "XTa%d" % s
                pk0, pk1 = "Pa%d" % pcur, "Pa%d" % (1 - pcur)
                for h in range(4):
                    mm(psE1_3[:, h, :], XT[:, h, :], Pa[pcur][:, h, :], True, False, [XTk, pk0], PSE1)
                    mm(psE1_3[:, h, :], ident_b[:], Pa[pcur][:, h, :], False, True, ["ident_b", pk0], PSE1)
                cp("dve", Pa[1 - pcur][:], psE1_3, PSE1, [pk1])
                pcur = 1 - pcur
                yield
            Tt, Tk = Pa[pcur], "Pa%d" % pcur
            if dbg0:
                dump("Tt", Tt[:], [128, 4, 128], BF16, [Tk])
            for h in range(4):
                mm(psG3[:, h, :], Tt[:, h, :], vtm[:, h, :], True, True, [Tk, "vtm"], ["psG"])
            for h in range(4):
                mm(psE0_3[:, h, :], kg[:, h, :], Tt[:, h, :], True, True, [Tk, "kg"], PSE0)
            tt_("dve", ub[:], psG3, esc[:, 8:12].unsqueeze(2).to_broadcast([128, 4, 128]), ALU.mult, ["psG", "esc"], ["ub"])
            cp("dve", wT[:], psE0_3, PSE0, ["wT"])
            yield
            chunks = (0, 1) if r == 0 else (1, 0)
            for ci in chunks:
                R = slice(64 * ci, 64 * ci + 64)
                lastcol = (64 * ci + 63) if r == 0 else (64 * ci)
                for h in range(4):
                    mm(psE0_3[:, h, :], wT[:, h, :], S_b[:, h, :], True, True, ["wT", "S_b%d" % h], ["psE0"])
                yield
                for h in range(4):
                    stt("dve", vnew[R, h, :], psE0_3[R, h, :], nbe[R, h:h + 1], ub[R, h, :], ALU.mult, ALU.add,
                        ["psE0", "nbe", "ub"], ["vnew%d" % h])
                for h in range(4):
                    mm(psD3[:, h, R], S_b[:, h, :], qgT[:, h, R], True, False, ["S_b%d" % h, "qgT"], ["psD"])
                    mm(psD3[:, h, R], vnew[R, h, :], attT[R, h, R], False, True, ["vnew%d" % h, "attT"], ["psD"])
                    mm(psE1_3[:, h, :], ktl[R, h, :], vnew[R, h, :], True, True, ["ktl", "vnew%d" % h], ["psE1"])
                for h in range(4):
                    stt("dve", S_f[:, h, :], S_f[:, h, :], EGb[:, h, lastcol:lastcol + 1], psE1_3[:, h, :], ALU.mult, ALU.add,
                        ["S_f%d" % h, "EGb", "psE1"], ["S_f%d" % h])
                    cp("dve", S_b[:, h, :], S_f[:, h, :], ["S_f%d" % h], ["S_b%d" % h])
                yield
            if r == 0:
                cp("dve", o1T[:, :, c0:c0 + 128], psD3, ["psD"], ["o1T"])
            else:
                tt_("dve", oacc[:, :, c0:c0 + 128], psD3, o1T[:, :, c0:c0 + 128], ALU.add, ["psD", "o1T"], ["oacc"])

        def sweep2_pre(tt):
            qb = tt % 2
            dma("sync", qkvT2[qb][:], scr_qkv[tt].rearrange("p (c t) -> p c t", c=12), ["scr_qkv%d" % tt], qkeys(qb), "ld_qkv%d" % qb)
            dma("sync", o1T[:], scr_o1[tt].rearrange("p (c t) -> p c t", c=4), ["scr_o1%d" % tt], ["o1T"], "ld_o1")
            dma("sync", ab_tm2[:, qb], scr_ab[tt].rearrange("p (b c) -> p b c", b=4), ["scr_ab%d" % tt], ["ab_tm%d" % qb], "ld_ab%d" % qb)
            return delta_tile(tt, 1)

        def sweep2_tile(l, tt, xsrc, xdst, lastlayer, nxt):
            t0 = tt * TT

            step = mk_step(nxt, 1.5)

            dma("sync", szT[:], scr_sz[tt].rearrange("p (c t) -> p c t", c=4), ["scr_sz%d" % tt], ["szT"], "ld_sz")
            dma("sync", mixT[:, 4:8, :], scr_ysg[tt].rearrange("p (c t) -> p c t", c=4), ["scr_ysg%d" % tt], ["mixT_sg"], "ld_ysg")
            load_x(xsrc, tt, False)
            if tt == 0:
                dump("oacc", oacc[:], [128, 4, TT], F32, ["oacc"])
            for h in range(4):
                act(sqs[:, 0, 0:TT], oacc[:, h, :], AF.Square, ["oacc"], ["sqs0"])
                mm(psQ[:, 0:TT], ones_b[:], sqs[:, 0, 0:TT], True, True, ["ones_b", "sqs0"], ["psQ"])
                rsqrt(tmp_s[:, 0:TT], psQ[:, 0:TT], ["psQ"], ["tmp_s"], bias=EPS, scale=1.0 / 128)
                ts_("pool", gt[:, 2, :], oacc[:, h, :], pp[:, PP_DNNG:PP_DNNG + 1], None, ALU.mult, None, ["oacc", "pp"], ["gt2"])
                tt_("pool", gt[:, 2, :], gt[:, 2, :], tmp_s[:, 0:TT], ALU.mult, ["gt2", "tmp_s"], ["gt2"])
                tt_("pool", mixT[:, h, :], gt[:, 2, :], szT[:, h, :], ALU.mult, ["gt2", "szT"], ["mixT_dn"])
                step()
            if tt == 0:
                dump("mixT", mixT[:], [128, 8, TT], BF16, ["mixT_dn", "mixT_sg"])
            for g in range(2):
                wv, wk = w_next("out", l, g * 512)
                for mi in range(4):
                    m = g * 4 + mi
                    ps, pk = dense_ps()
                    for k in range(KD):
                        mm(ps[:, 0:TT], wv[:, k, mi * 128:(mi + 1) * 128], mixT[:, k, :], k == 0, k == KD - 1,
                           [wk, "mixT_dn", "mixT_sg"], [pk])
                    sx = m % 2
                    cp("act", gt[:, sx, :], ps[:, 0:TT], [pk], ["gt%d" % sx])
                    tt_("pool", xt[:, m, 0:TT], xt[:, m, 0:TT], gt[:, sx, :], ALU.add, ["xt", "gt%d" % sx], ["xt"])
                    step()
            if tt == 0:
                dump("xmid", xt[:, :, 0:TT], [128, KD, TT], F32, ["xt"])
            norm_stage(PP_GFFN, False, False)
            for g in range(6):
                n = 4 if g < 5 else 2
                wg, wgk = w_next("gate", l, g * 512)
                wu, wuk = w_next("up", l, g * 512)
                for mi in range(n):
                    m = g * 4 + mi
                    pg, pgk = dense_ps()
                    pu, puk = dense_ps()
                    for k in range(KD):
                        mm(pg[:, 0:TT], wg[:, k, mi * 128:(mi + 1) * 128], hT[:, k, 0:TT], k == 0, k == KD - 1, [wgk, "hT"], [pgk])
                    for k in range(KD):
                        mm(pu[:, 0:TT], wu[:, k, mi * 128:(mi + 1) * 128], hT[:, k, 0:TT], k == 0, k == KD - 1, [wuk, "hT"], [puk])
                    s = m % 2
                    act(cv[:, s, :], pg[:, 0:TT], AF.Silu, [pgk], ["cv%d" % s])
                    cp("act", gt[:, s, :], pu[:, 0:TT], [puk], ["gt%d" % s])
                    tt_("pool", hidT[:, m, :], cv[:, s, :], gt[:, s, :], ALU.mult, ["cv%d" % s, "gt%d" % s], ["hidT"])
                    step()
            for m in range(8):
                wv, wk = w_next("down", l, m * 128)
                ps, pk = dense_ps()
                for k in range(KF):
                    mm(ps[:, 0:TT], wv[:, k, :], hidT[:, k, :], k == 0, k == KF - 1, [wk, "hidT"], [pk])
                sx = m % 2
                cp("act", gt[:, sx, :], ps[:, 0:TT], [pk], ["gt%d" % sx])
                tt_("pool", xt[:, m, 0:TT], xt[:, m, 0:TT], gt[:, sx, :], ALU.add, ["xt", "gt%d" % sx], ["xt"])
                step()
            if tt == 0:
                dump("hid", hidT[:], [128, KF, TT], BF16, ["hidT"])
                dump("xfin", xt[:, :, 0:TT], [128, KD, TT], F32, ["xt"])
            if not lastlayer:
                dma("sync", xdst[0][:, :, t0:t0 + TT], xt[:, :, 0:TT], ["xt"], ["%s_%d" % (xdst[1], tt)], "st_x")
            else:
                for k in range(KD):
                    s = k % 2
                    act(sqs[:, s, 0:TT], xt[:, k, 0:TT], AF.Square, ["xt"], ["sqs%d" % s])
                    mm(psQ[:, 0:TT], ones_b[:], sqs[:, s, 0:TT], k == 0, k == KD - 1, ["ones_b", "sqs%d" % s], ["psQ"])
                rsqrt(rstd_b[:, 0:TT], psQ[:, 0:TT], ["psQ"], ["rstd_b"], bias=1024 * EPS)
                for k in range(KD):
                    s = k % 2
                    ts_("pool", cv[:, s, :], xt[:, k, 0:TT], fg[:, k:k + 1], None, ALU.mult, None, ["xt", "fg"], ["cv%d" % s])
                    tt_("pool", cv[:, s, :], cv[:, s, :], rstd_b[:, 0:TT], ALU.mult, ["cv%d" % s, "rstd_b"], ["cv%d" % s])
                    dma("sync", out_d[:, k, t0:t0 + TT], cv[:, s, :], ["cv%d" % s], ["out%d" % k], "st_out%d" % s)

        for l in range(L):
            load_layer_params(l)
            xsrc = (xin, "xin") if l == 0 else (xs1, "xs1")
            xdst = (xs1, "xs1")
            lastlayer = l == L - 1
            for h in range(4):
                memset("dve", S_f[:, h, :], 0.0, ["S_f%d" % h])
                memset("dve", S_b[:, h, :], 0.0, ["S_b%d" % h])
            P.op("dve", lambda h: h.memset(dummy[:], 0.0), [], ["hidT"] + ["pbuf%d" % j for j in range(12)])
            proj_tile(l, 0, xsrc, None)
            for tt in range(NTT):
                g1 = delta1_gen(tt)
                if tt + 1 < NTT:
                    proj_tile(l, tt + 1, xsrc, g1)
                for _ in g1:
                    pass
            for h in range(4):
                memset("dve", S_f[:, h, :], 0.0, ["S_f%d" % h])
                memset("dve", S_b[:, h, :], 0.0, ["S_b%d" % h])
            P.op("dve", lambda h: h.memset(dummy[:], 0.0), [], ["hidT"] + ["pbuf%d" % j for j in range(12)])
            gen = sweep2_pre(NTT - 1)
            for _ in gen:
                pass
            for tt in range(NTT - 1, -1, -1):
                nxt = sweep2_pre(tt - 1) if tt > 0 else None
                sweep2_tile(l, tt, xsrc, xdst, lastlayer, nxt)
                if nxt is not None:
                    for _ in nxt:
                        pass
        P.op("sync", lambda h: h.nop(), ["out%d" % k for k in range(KD)] + ["dbg_" + n for n in dbg_outs], [])
        P.emit(st)
    return nc


def _consts():
    i = np.arange(128)
    same = (i[:, None] // 64) == (i[None, :] // 64)
    c = np.zeros((128, CS_N), np.float32)
    c[:, CS_ID:CS_ID + 128] = np.eye(128)
    c[:, CS_MC0:CS_MC0 + 128] = same & (i[:, None] <= i[None, :])
    c[:, CS_MC1:CS_MC1 + 128] = same & (i[:, None] >= i[None, :])
    c[:, CS_BD:CS_BD + 128] = same
    c[:, CS_NM0:CS_NM0 + 128] = np.where(same & (i[None, :] >= i[:, None]), 0.0, -30000.0)
    c[:, CS_NM1:CS_NM1 + 128] = np.where(same & (i[None, :] <= i[:, None]), 0.0, -30000.0)
    c[:, CS_ST0:CS_ST0 + 128] = same & (i[None, :] > i[:, None])
    c[:, CS_ST1:CS_ST1 + 128] = same & (i[None, :] < i[:, None])
    return c


def _host_params(inp, L):
    f = np.float32
    pp = np.zeros((128, L, PP_N), f)
    bp = np.zeros((128, L, BP_N), f)
    sgwT = np.zeros((L, 128, 512), f)
    for l in range(L):
        pp[:, l, PP_GMIX:PP_GMIX + 8] = np.asarray(inp["mix_norm_g"][l]).reshape(8, 128).T
        pp[:, l, PP_GFFN:PP_GFFN + 8] = np.asarray(inp["ffn_norm_g"][l]).reshape(8, 128).T
        cw = np.asarray(inp["conv_w"][l])
        pp[:, l, PP_CONV:PP_CONV + 36] = cw.reshape(3, 12, 128).transpose(2, 1, 0).reshape(128, 36)
        pp[:, l, PP_DNNG] = np.asarray(inp["dn_norm_g"][l])
        pp[:, l, PP_SGOG:PP_SGOG + 4] = np.asarray(inp["sg_out_g"][l]).reshape(4, 128).T
        bp[:, l, BP_ALOG:BP_ALOG + 8] = np.asarray(inp["dn_a_log"][l]).reshape(8)[None, :]
        bp[:, l, BP_OFF:BP_OFF + 8] = np.asarray(inp["dn_dt_bias"][l]).reshape(8)[None, :]
        bp[:, l, BP_SGN:BP_SGN + 8] = 1.0
        bp[:, l, BP_SGN + 8:BP_SGN + 16] = -1.0
        bp[:, l, BP_LNG:BP_LNG + 512] = np.asarray(inp["sg_ln_g"][l])[None, :]
        bp[:, l, BP_LNB:BP_LNB + 512] = np.asarray(inp["sg_ln_b"][l])[None, :]
        bp[:, l, BP_BSB:BP_BSB + 512] = np.asarray(inp["sg_b"][l]).reshape(512)[None, :]
        sgwT[l] = np.asarray(inp["sg_w"][l]).transpose(2, 0, 1).reshape(128, 512)
    fgv = np.ascontiguousarray(np.asarray(inp["final_norm_g"]).reshape(8, 128).T).astype(f)
    return pp, bp, sgwT, fgv


_NC_CACHE = {}


def run(inp, dbg=None, trace=False):
    x = np.asarray(inp["x"])
    B, S, _ = x.shape
    L = np.asarray(inp["w_in"]).shape[0]
    key = (S, L, tuple(sorted(dbg or ())))
    if key not in _NC_CACHE:
        _NC_CACHE[key] = build(S, L, dbg)
    nc = _NC_CACHE[key]
    pp, bp, sgwT, fgv = _host_params(inp, L)
    cs = _consts()
    shared = {
        "w_in": np.ascontiguousarray(inp["w_in"], np.float32), "w_out": np.ascontiguousarray(inp["w_out"], np.float32),
        "w_gate": np.ascontiguousarray(inp["w_gate"], np.float32), "w_up": np.ascontiguousarray(inp["w_up"], np.float32),
        "w_down": np.ascontiguousarray(inp["w_down"], np.float32),
        "pp": pp, "fg": fgv, "bp": bp, "sgwT": sgwT, "consts": cs,
    }
    in_maps = []
    for b in range(B):
        xi = np.zeros((128, KD, S + 1), np.float32)
        xi[:, :, 0:S] = x[b].reshape(S, KD, 128).transpose(2, 1, 0)
        m = dict(shared)
        m["xin"] = xi
        in_maps.append(m)
    res = run_bass_kernel_spmd(nc, in_maps, core_ids=list(range(B)), **({"trace": True} if trace else {}))
    out = np.zeros((B, S, D), np.float32)
    for b in range(B):
        out[b] = res.results[b]["out"].transpose(2, 1, 0).reshape(S, D)
    return out, res


def kernel(**inputs):
    out, _ = run(inputs)
    return out
```

```python
import math
from contextlib import ExitStack

import numpy as np

import concourse.bass as bass
import concourse.mybir as mybir
from concourse.bass_utils import run_bass_kernel_spmd

F32 = mybir.dt.float32
BF16 = mybir.dt.bfloat16
AF = mybir.ActivationFunctionType
ALU = mybir.AluOpType

D = 1024
KD = 8
PW = 3088
FF = 2816
KF = 22
TT = 512
EPS = 1e-6
C_Q, C_K, C_V, C_Z, C_AB, C_U, C_VS = 0, 512, 1024, 1536, 2048, 2064, 2576
QSCALE = 128.0 ** -0.5
GELU_C = 1.5957691216057308

PP_GMIX, PP_GFFN, PP_CONV, PP_DNNG, PP_SGOG, PP_N = 0, 8, 16, 52, 53, 57
BP_ALOG, BP_OFF, BP_SGN, BP_LNG, BP_LNB, BP_BSB, BP_N = 0, 16, 32, 48, 560, 1072, 1584
CS_ID, CS_MC0, CS_MC1, CS_BD, CS_NM0, CS_NM1, CS_ST0, CS_ST1, CS_N = 0, 128, 256, 384, 512, 640, 768, 896, 1024


class _I:
    __slots__ = ("fn", "deps", "dma", "dkey", "val", "needs_inc")

    def __init__(self, fn, deps, dma, dkey):
        self.fn = fn
        self.deps = deps
        self.dma = dma
        self.dkey = dkey
        self.val = 0
        self.needs_inc = False


class Prog:
    ENGS = ("sync", "act", "dve", "pool", "pe")

    def __init__(self, nc):
        self.nc = nc
        self.ins = {e: [] for e in self.ENGS}
        self.last_w = {}
        self.readers = {}
        self.dcount = {}

    def op(self, eng, fn, reads=(), writes=(), dkey=None):
        deps = set()
        for k in reads:
            lw = self.last_w.get(k)
            if lw is not None:
                deps.add(lw)
        for k in writes:
            lw = self.last_w.get(k)
            if lw is not None:
                deps.add(lw)
            for r in self.readers.get(k, ()):
                deps.add(r)
        idx = len(self.ins[eng])
        dma = dkey is not None
        if eng == "pe" and not dma:
            deps = {d for d in deps if d[0] != "pe"}
        it = _I(fn, deps, dma, dkey)
        if dma:
            c = self.dcount.get(dkey, 0) + 16
            self.dcount[dkey] = c
            it.val = c
        self.ins[eng].append(it)
        me = (eng, idx)
        for k in reads:
            lst = self.readers.setdefault(k, [])
            if not dma:
                lst[:] = [r for r in lst if not (r[0] == eng and not self.ins[r[0]][r[1]].dma)]
            lst.append(me)
        for k in writes:
            self.last_w[k] = me
            self.readers[k] = []
        return me

    def emit(self, stack):
        nc = self.nc
        for e in self.ENGS:
            for it in self.ins[e]:
                for (de, di) in it.deps:
                    self.ins[de][di].needs_inc = True
        csem = {}
        for e in self.ENGS:
            csem[e] = stack.enter_context(nc.semaphore("c_" + e))
            n = 0
            for it in self.ins[e]:
                if not it.dma and it.needs_inc:
                    n += 1
                    it.val = n
        dsem = {}
        for k in self.dcount:
            dsem[k] = stack.enter_context(nc.semaphore("d_" + str(k)))
        block = stack.enter_context(nc.Block())
        ins = self.ins

        def run(e, h):
            waited = {}
            for it in ins[e]:
                need = {}
                for (de, di) in it.deps:
                    d = ins[de][di]
                    s = dsem[d.dkey] if d.dma else csem[de]
                    key = id(s)
                    if waited.get(key, 0) >= d.val:
                        continue
                    if key not in need or need[key][1] < d.val:
                        need[key] = (s, d.val)
                for key, (s, v) in need.items():
                    h.wait_ge(s, v)
                    waited[key] = v
                r = it.fn(h)
                if it.dma:
                    r.then_inc(dsem[it.dkey], 16)
                elif it.needs_inc:
                    r.then_inc(csem[e], 1)

        @block.sync
        def _(h):
            run("sync", h)

        @block.scalar
        def _(h):
            run("act", h)

        @block.vector
        def _(h):
            run("dve", h)

        @block.gpsimd
        def _(h):
            run("pool", h)

        @block.tensor
        def _(h):
            run("pe", h)


def build(NT, L, dbg=None):
    NTT = NT // TT
    dbg = dbg or set()
    nc = bass.Bass("TRN2", target_bir_lowering=False)

    def din(name, shape, dt=F32):
        return nc.dram_tensor(name, list(shape), dt, kind="ExternalInput").ap()

    xin = din("xin", [128, KD, NT + 1])
    w_in = din("w_in", [L, D, PW])
    w_out = din("w_out", [L, D, D])
    w_gate = din("w_gate", [L, D, FF])
    w_up = din("w_up", [L, D, FF])
    w_down = din("w_down", [L, FF, D])
    pp_d = din("pp", [128, L, PP_N])
    fg_d = din("fg", [128, KD])
    bp_d = din("bp", [128, L, BP_N])
    sgw_d = din("sgwT", [L, 128, 512])
    cs_d = din("consts", [128, CS_N])
    out_d = nc.dram_tensor("out", [128, KD, NT], F32, kind="ExternalOutput").ap()

    xs1 = nc.dram_tensor("xs1", [128, KD, NT + 1], F32, kind="Internal").ap()
    scr_qkv = nc.dram_tensor("scr_qkv", [NTT, 128, 12 * TT], BF16, kind="Internal").ap()
    scr_sz = nc.dram_tensor("scr_sz", [NTT, 128, 4 * TT], BF16, kind="Internal").ap()
    scr_ysg = nc.dram_tensor("scr_ysg", [NTT, 128, 4 * TT], BF16, kind="Internal").ap()
    scr_o1 = nc.dram_tensor("scr_o1", [NTT, 128, 4 * TT], BF16, kind="Internal").ap()
    scr_ab = nc.dram_tensor("scr_ab", [NTT, 128, 64], F32, kind="Internal").ap()

    dbg_outs = {}

    with ExitStack() as st:
        def sb(name, shape, dt=F32):
            return st.enter_context(nc.sbuf_tensor("s_" + name, list(shape), dt))

        def pst(name, shape, dt=F32):
            return st.enter_context(nc.psum_tensor("p_" + name, list(shape), dt))

        cs = sb("cs", [128, CS_N])
        ident_b = sb("ident_b", [128, 128], BF16)
        ones_b = sb("ones_b", [128, 128], BF16)
        ones_f = sb("ones_f", [128, 128])
        negm_b = sb("negm_b", [128, 2, 512], BF16)
        st01 = sb("st01", [128, 2, 512])
        zero_f = sb("zero_f", [128, 8])
        pp = sb("pp", [128, PP_N])
        g32 = sb("g32", [128, 16])
        fg = sb("fg", [128, KD])
        bp = sb("bp", [128, BP_N])
        mul16 = sb("mul16", [128, 16])
        wsT = sb("wsT", [128, 512], BF16)
        wab = sb("wab", [128, KD, 16], BF16)
        xt = sb("xt", [128, KD, TT + 1])
        hT = sb("hT", [128, KD, TT + 1], BF16)
        sqs = sb("sqs", [128, 2, TT + 1], BF16)
        rstd_b = sb("rstd_b", [128, TT + 1])
        rstd_tm = sb("rstd_tm", [128, 4])
        tmp_s = sb("tmp_s", [128, TT + 8])
        NWS = 4
        wsl = [sb("wsl%d" % i, [128, 4096], BF16) for i in range(NWS)]
        arena = sb("arena", [128, 12 * (TT + 2)])
        pbuf = arena[:, :].rearrange("p (c t) -> p c t", c=12)
        hidT = arena[:, 0:KF * TT // 2].bitcast(BF16).rearrange("p (k t) -> p k t", k=KF)
        dummy = sb("dummy", [128, 2])
        carry = sb("carry", [128, 12])
        cv = sb("cv", [128, 2, TT])
        uT = sb("uT", [128, 4, TT])
        gt = sb("gt", [128, 3, TT])
        vln = sb("vln", [128, 4, 512], BF16)
        qkvT2 = [sb("qkvT_%d" % i, [128, 12, TT], BF16) for i in range(2)]
        szT = sb("szT", [128, 4, TT], BF16)
        mixT = sb("mixT", [128, 8, TT], BF16)
        o1T = sb("o1T", [128, 4, TT], BF16)
        oacc = sb("oacc", [128, 4, TT])
        ab_tm2 = sb("ab_tm", [128, 2, 4, 16])
        stats = sb("stats", [128, 8])
        stats2 = sb("stats2", [128, 2])
        sqh = sb("sqh", [128, 8], BF16)
        t16 = sb("t16", [128, 4, 16])
        G16 = sb("G16", [128, 4, 16])
        sc12 = sb("sc12", [128, 12])
        esc = sb("esc", [128, 12])
        ngc = sb("ngc", [128, 4])
        nbe = sb("nbe", [128, 4])
        rhsP = sb("rhsP", [128, 4, 128])
        decT = sb("decT", [128, 4, 128])
        EGb = sb("EGb", [128, 4, 128])
        Ebs = sb("Ebs", [128, 4, 128])
        qgT = sb("qgT", [128, 4, 128], BF16)
        kg = sb("kg", [128, 4, 128], BF16)
        ktl = sb("ktl", [128, 4, 128], BF16)
        vtm = sb("vtm", [128, 4, 128], BF16)
        Nm = sb("Nm", [128, 4, 128], BF16)
        NmT = sb("NmT", [128, 4, 128], BF16)
        Xa = [sb("Xa%d" % i, [128, 4, 128], BF16) for i in range(2)]
        XTa = [sb("XTa%d" % i, [128, 4, 128], BF16) for i in range(2)]
        Pa = [sb("Pa%d" % i, [128, 4, 128], BF16) for i in range(2)]
        attT = sb("attT", [128, 4, 128], BF16)
        ub = sb("ub", [128, 4, 128])
        wT = sb("wT", [128, 4, 128], BF16)
        vnew = sb("vnew", [128, 4, 128], BF16)
        S_f = sb("S_f", [128, 4, 128])
        S_b = sb("S_b", [128, 4, 128], BF16)

        psA = pst("psA", [128, 512])
        psB = pst("psB", [128, 512])
        psQ = pst("psQ", [128, 512])
        psD = pst("psD", [128, 512])
        psE = pst("psE", [128, 1024])
        psG = pst("psG", [128, 512])
        psTS = pst("psTS", [128, 512])
        psT = psTS[:, 0:256].bitcast(BF16).rearrange("p (h i) -> p h i", h=4)
        psS = psTS[:, 384:512]

        P = Prog(nc)

        def mm(out, lhsT, rhs, start, stop, r, w):
            P.op("pe", lambda h: h.matmul(out, lhsT=lhsT, rhs=rhs, start=start, stop=stop), r, w)

        def tr(out, in_, r, w):
            P.op("pe", lambda h: h.transpose(out, in_, ident_b[:]), list(r) + ["ident_b"], w)

        def tt_(eng, out, in0, in1, op, r, w):
            P.op(eng, lambda h: h.tensor_tensor(out=out, in0=in0, in1=in1, op=op), r, w)

        def ts_(eng, out, in0, s1, s2, op0, op1, r, w):
            if s2 is None:
                P.op(eng, lambda h: h.tensor_scalar(out=out, in0=in0, scalar1=s1, scalar2=None, op0=op0), r, w)
            else:
                P.op(eng, lambda h: h.tensor_scalar(out=out, in0=in0, scalar1=s1, scalar2=s2, op0=op0, op1=op1), r, w)

        def stt(eng, out, in0, scalar, in1, op0, op1, r, w):
            P.op(eng, lambda h: h.scalar_tensor_tensor(out=out, in0=in0, scalar=scalar, in1=in1, op0=op0, op1=op1), r, w)

        def act(out, in_, func, r, w, bias=0.0, scale=1.0):
            P.op("act", lambda h: h.activation(out=out, in_=in_, func=func, bias=bias, scale=scale), r, w)

        def cp(eng, out, in_, r, w):
            if eng == "act":
                P.op("act", lambda h: h.copy(out=out, in_=in_), r, w)
            else:
                P.op(eng, lambda h: h.tensor_copy(out=out, in_=in_), r, w)

        def recip(out, in_, r, w):
            P.op("dve", lambda h: h.reciprocal(out=out, in_=in_), r, w)

        def dma(eng, out, in_, r, w, dkey, slow=False):
            if slow:
                P.op(eng, lambda h: h.dma_start(out=out, in_=in_, allow_slow_non_contiguous=True), r, w, dkey=dkey)
            else:
                P.op(eng, lambda h: h.dma_start(out=out, in_=in_), r, w, dkey=dkey)

        def memset(eng, ap, val, w):
            P.op(eng, lambda h: h.memset(ap, val), (), w)

        def dump(name, ap, shape, dt, rkeys):
            if name not in dbg:
                return
            o = nc.dram_tensor("dbg_" + name, list(shape), dt, kind="ExternalOutput").ap()
            dbg_outs[name] = o
            dma("sync", o, ap, rkeys, ["dbg_" + name], "dbg_" + name)

        def rsqrt(out, in_, r, w, bias, scale=1.0):
            act(out, in_, AF.Ln, r, w, bias=bias, scale=scale)
            act(out, out, AF.Exp, w, w, scale=-0.5)

        plan = []
        for l in range(L):
            for tt in range(NTT):
                for nm, c0 in (("q", C_Q), ("k", C_K), ("v", C_V), ("z", C_Z), ("u", C_U), ("vs", C_VS)):
                    plan.append(("in", l, c0, 512))
            for tt in range(NTT):
                for g in range(2):
                    plan.append(("out", l, g * 512, 512))
                for g in range(6):
                    n = 512 if g < 5 else 256
                    plan.append(("gate", l, g * 512, n))
                    plan.append(("up", l, g * 512, n))
                for m in range(8):
                    plan.append(("down", l, m * 128, 128))
        wstate = {"issued": 0, "used": 0}

        def w_issue(i):
            kind, l, c0, n = plan[i]
            slot = i % NWS
            key = "wsl%d" % slot
            if kind == "down":
                src = w_down[l].rearrange("(k p) c -> p k c", p=128)[:, :, c0:c0 + n]
                dst = wsl[slot][:, 0:KF * n].rearrange("p (k c) -> p k c", k=KF)
            else:
                wsrc = {"in": w_in, "out": w_out, "gate": w_gate, "up": w_up}[kind]
                src = wsrc[l].rearrange("(k p) c -> p k c", p=128)[:, :, c0:c0 + n]
                dst = wsl[slot][:, 0:KD * n].rearrange("p (k c) -> p k c", k=KD)
            dma("pool", dst, src, [], [key], key)

        def w_next(kind, l, c0):
            i = wstate["used"]
            assert plan[i][0] == kind and plan[i][1] == l and plan[i][2] == c0, (plan[i], kind, l, c0)
            while wstate["issued"] < min(len(plan), i + NWS - 1):
                w_issue(wstate["issued"])
                wstate["issued"] += 1
            wstate["used"] += 1
            slot = i % NWS
            n = plan[i][3]
            k = KF if kind == "down" else KD
            return wsl[slot][:, 0:k * n].rearrange("p (k c) -> p k c", k=k), "wsl%d" % slot

        dma("sync", cs[:], cs_d[:], [], ["cs"], "cs")
        cp("dve", ident_b[:], cs[:, CS_ID:CS_ID + 128], ["cs"], ["ident_b"])
        memset("dve", ones_b[:], 1.0, ["ones_b"])
        memset("dve", ones_f[:], 1.0, ["ones_f"])
        memset("dve", zero_f[:], 0.0, ["zero_f"])
        for r_ in range(2):
            c_nm = CS_NM0 if r_ == 0 else CS_NM1
            c_st = CS_ST0 if r_ == 0 else CS_ST1
            cp("dve", negm_b[:, r_, :].rearrange("p (h i) -> p h i", h=4),
               cs[:, c_nm:c_nm + 128].unsqueeze(1).to_broadcast([128, 4, 128]), ["cs"], ["negm_b"])
            cp("dve", st01[:, r_, :].rearrange("p (h i) -> p h i", h=4),
               cs[:, c_st:c_st + 128].unsqueeze(1).to_broadcast([128, 4, 128]), ["cs"], ["st01"])
        dma("sync", fg[:], fg_d[:], [], ["fg"], "fg")
        ts_("dve", fg[:], fg[:], 32.0, None, ALU.mult, None, ["fg"], ["fg"])
        dma("sync", xs1[:, :, NT:NT + 1], zero_f[:, :].unsqueeze(2), ["zero_f"], ["xs1_%d" % NTT], "xs1h", slow=True)
        Mc = [cs[:, CS_MC0:CS_MC0 + 128], cs[:, CS_MC1:CS_MC1 + 128]]
        BD = cs[:, CS_BD:CS_BD + 128]

        def load_layer_params(l):
            dma("sync", pp[:], pp_d[:, l, :], [], ["pp"], "pp")
            ts_("dve", g32[:], pp[:, 0:16], 32.0, None, ALU.mult, None, ["pp"], ["g32"])
            dma("sync", bp[:], bp_d[:, l, :], [], ["bp"], "bp")
            act(mul16[:], bp[:, BP_ALOG:BP_ALOG + 16], AF.Exp, ["bp"], ["mul16"])
            ts_("dve", mul16[:], mul16[:], -1.0, None, ALU.mult, None, ["mul16"], ["mul16"])
            dma("pool", wsT[:], sgw_d[l], [], ["wsT"], "wsT")
            dma("pool", wab[:], w_in[l].rearrange("(k p) c -> p k c", p=128)[:, :, C_AB:C_AB + 16], [], ["wab"], "wab")

        def load_x(src, tt, halo):
            t0 = tt * TT
            n = TT + 1 if halo else TT
            rk = ["%s_%d" % (src[1], tt)] + (["%s_%d" % (src[1], tt + 1)] if halo else [])
            dma("sync", xt[:, :, 0:n], src[0][:, :, t0:t0 + n], rk, ["xt"], "xt")

        dense_ctr = [0]

        def dense_ps():
            i = dense_ctr[0]
            dense_ctr[0] += 1
            return (psA, "psA") if i % 2 == 0 else (psB, "psB")

        def norm_stage(gcol, halo, tokmajor):
            n = TT + 1 if halo else TT
            for k in range(KD):
                s = k % 2
                act(sqs[:, s, 0:TT], xt[:, k, 0:TT], AF.Square, ["xt"], ["sqs%d" % s])
                mm(psQ[:, 0:TT], ones_b[:], sqs[:, s, 0:TT], k == 0, k == KD - 1, ["ones_b", "sqs%d" % s], ["psQ"])
            rsqrt(rstd_b[:, 0:TT], psQ[:, 0:TT], ["psQ"], ["rstd_b"], bias=1024 * EPS)
            rk = ["rstd_b"]
            if halo:
                act(sqh[:, 0:8], xt[:, :, TT], AF.Square, ["xt"], ["sqh"])
                psh, phk = dense_ps()
                mm(psh[:, 0:8], ones_b[:], sqh[:, 0:8], True, True, ["ones_b", "sqh"], [phk])
                P.op("dve", lambda h: h.reduce_sum(out=stats2[:, 0:1], in_=psh[:, 0:8], axis=mybir.AxisListType.X),
                     [phk], ["stats2"])
                rsqrt(rstd_b[:, TT:TT + 1], stats2[:, 0:1], ["stats2"], ["rstd_bh"], bias=1024 * EPS)
                rk = ["rstd_b", "rstd_bh"]
            for k in range(KD):
                stt("dve", hT[:, k, 0:n], xt[:, k, 0:n], g32[:, gcol + k:gcol + k + 1], rstd_b[:, 0:n], ALU.mult, ALU.mult,
                    ["xt", "g32"] + rk, ["hT"])

        def gelu(x_ap, out_ap, xk, ok, tslot):
            a = gt[:, tslot, :]
            ak = "gt%d" % tslot
            tt_("dve", a, x_ap, x_ap, ALU.mult, xk, [ak])
            ts_("dve", a, a, 0.044715, 1.0, ALU.mult, ALU.add, [ak], [ak])
            tt_("dve", a, a, x_ap, ALU.mult, [ak] + list(xk), [ak])
            act(a, a, AF.Sigmoid, [ak], [ak], scale=GELU_C)
            tt_("dve", out_ap, x_ap, a, ALU.mult, [ak] + list(xk), ok)

        def qkeys(qb):
            return ["qkvT%d_%d" % (qb, j) for j in range(12)]

        def mk_step(gen, ratio):
            acc = [0.0]

            def step():
                if gen is None:
                    return
                acc[0] += ratio
                while acc[0] >= 1.0:
                    acc[0] -= 1.0
                    try:
                        next(gen)
                    except StopIteration:
                        return
            return step

        def proj_tile(l, tt, xsrc, gen):
            step = mk_step(gen, 0.0)
            first = tt == 0
            qb = tt % 2
            qkvT = qkvT2[qb]
            ab_tm = ab_tm2[:, qb]
            load_x(xsrc, tt, True)
            norm_stage(PP_GMIX, True, True)
            if first:
                dump("hT", hT[:], [128, KD, TT + 1], BF16, ["hT"])
                dump("rstd_b", rstd_b[:], [128, TT + 1], F32, ["rstd_b", "rstd_bh"])
                memset("dve", carry[:], 0.0, ["carry0", "carry1", "carry2"])
            step()
            for gi, c0 in enumerate((C_Q, C_K, C_V)):
                wv, wk = w_next("in", l, c0)
                for mi in range(4):
                    j = gi * 4 + mi
                    ps, pk = dense_ps()
                    for k in range(KD):
                        mm(ps[:, 0:TT], wv[:, k, mi * 128:(mi + 1) * 128], hT[:, k, 0:TT], k == 0, k == KD - 1, [wk, "hT"], [pk])
                    cp("act", pbuf[:, j, 1:TT + 1], ps[:, 0:TT], [pk], ["pbuf%d" % j])
                psh, phk = dense_ps()
                for mi in range(4):
                    for k in range(KD):
                        mm(psh[:, mi:mi + 1], wv[:, k, mi * 128:(mi + 1) * 128], hT[:, k, TT:TT + 1], k == 0, k == KD - 1,
                           [wk, "hT"], [phk])
                jk = ["pbuf%d" % j for j in range(gi * 4, gi * 4 + 4)]
                cp("dve", pbuf[:, gi * 4:gi * 4 + 4, 0], carry[:, gi * 4:gi * 4 + 4], ["carry%d" % gi], jk)
                cp("dve", pbuf[:, gi * 4:gi * 4 + 4, TT + 1], psh[:, 0:4], [phk], jk)
                cp("dve", carry[:, gi * 4:gi * 4 + 4], pbuf[:, gi * 4:gi * 4 + 4, TT], jk, ["carry%d" % gi])
                for mi in range(4):
                    j = gi * 4 + mi
                    s = j % 2
                    ck = "cv%d" % s
                    pk_ = "pbuf%d" % j
                    act(cv[:, s, :], pbuf[:, j, 1:TT + 1], AF.Copy, [pk_, "pp"], [ck], scale=pp[:, PP_CONV + 3 * j + 1:PP_CONV + 3 * j + 2])
                    stt("dve", cv[:, s, :], pbuf[:, j, 0:TT], pp[:, PP_CONV + 3 * j:PP_CONV + 3 * j + 1], cv[:, s, :], ALU.mult, ALU.add,
                        [pk_, "pp", ck], [ck])
                    stt("dve", cv[:, s, :], pbuf[:, j, 2:TT + 2], pp[:, PP_CONV + 3 * j + 2:PP_CONV + 3 * j + 3], cv[:, s, :], ALU.mult, ALU.add,
                        [pk_, "pp", ck], [ck])
                    if j >= 8:
                        act(qkvT[:, j, :], cv[:, s, :], AF.Silu, [ck], ["qkvT%d_%d" % (qb, j)])
                    else:
                        act(cv[:, s, :], cv[:, s, :], AF.Silu, [ck], [ck])
                        act(sqs[:, s, 0:TT], cv[:, s, :], AF.Square, [ck], ["sqs%d" % s])
                        mm(psQ[:, 0:TT], ones_b[:], sqs[:, s, 0:TT], True, True, ["ones_b", "sqs%d" % s], ["psQ"])
                        rsqrt(tmp_s[:, 0:TT], psQ[:, 0:TT], ["psQ"], ["tmp_s"], bias=EPS)
                        tt_("dve", qkvT[:, j, :], cv[:, s, :], tmp_s[:, 0:TT], ALU.mult, [ck, "tmp_s"], ["qkvT%d_%d" % (qb, j)])
                    step()
            if first:
                dump("pbuf", pbuf[:, :, :], [128, 12, TT + 2], F32, ["pbuf%d" % j for j in range(12)])
                dump("qkvT", qkvT[:], [128, 12, TT], BF16, qkeys(qb))
            dma("sync", scr_qkv[tt].rearrange("p (c t) -> p c t", c=12), qkvT[:], qkeys(qb), ["scr_qkv%d" % tt], "st_qkv%d" % qb)
            wv, wk = w_next("in", l, C_Z)
            for mi in range(4):
                ps, pk = dense_ps()
                for k in range(KD):
                    mm(ps[:, 0:TT], wv[:, k, mi * 128:(mi + 1) * 128], hT[:, k, 0:TT], k == 0, k == KD - 1, [wk, "hT"], [pk])
                act(szT[:, mi, :], ps[:, 0:TT], AF.Silu, [pk], ["szT"])
                step()
            dma("sync", scr_sz[tt].rearrange("p (c t) -> p c t", c=4), szT[:], ["szT"], ["scr_sz%d" % tt], "st_sz")
            wv, wk = w_next("in", l, C_U)
            for mi in range(4):
                ps, pk = dense_ps()
                for k in range(KD):
                    mm(ps[:, 0:TT], wv[:, k, mi * 128:(mi + 1) * 128], hT[:, k, 0:TT], k == 0, k == KD - 1, [wk, "hT"], [pk])
                cp("act", gt[:, 1, :], ps[:, 0:TT], [pk], ["gt1"])
                gelu(gt[:, 1, :], uT[:, mi, :], ["gt1"], ["uT%d" % mi], 0)
                step()
            psh, phk = dense_ps()
            for b in range(4):
                for k in range(KD):
                    mm(psh[:, 16 * b:16 * b + 16], hT[:, k, b * 128:(b + 1) * 128], wab[:, k, :], k == 0, k == KD - 1,
                       ["wab", "hT"], [phk])
            cp("dve", ab_tm, psh[:, 0:64].rearrange("p (b c) -> p b c", b=4), [phk], ["ab_tm%d" % qb])
            dma("sync", scr_ab[tt].rearrange("p (b c) -> p b c", b=4), ab_tm, ["ab_tm%d" % qb], ["scr_ab%d" % tt], "st_ab%d" % qb)
            if first:
                dump("ab_tm", ab_tm, [128, 4, 16], F32, ["ab_tm%d" % qb])
            wv, wk = w_next("in", l, C_VS)
            for b in range(4):
                ps, pk = dense_ps()
                for k in range(KD):
                    mm(ps[:, 0:512], hT[:, k, b * 128:(b + 1) * 128], wv[:, k, :], k == 0, k == KD - 1, [wk, "hT"], [pk])
                cp("act", gt[:, 1, :], ps[:, 0:512], [pk], ["gt1"])
                gelu(gt[:, 1, :], gt[:, 2, :], ["gt1"], ["gt2"], 0)
                P.op("dve", lambda h: h.bn_stats(out=stats[:, 0:6], in_=gt[:, 2, :]), ["gt2"], ["stats"])
                P.op("dve", lambda h: h.bn_aggr(out=stats[:, 6:8], in_=stats[:, 0:6]), ["stats"], ["stats"])
                rsqrt(stats[:, 7:8], stats[:, 7:8], ["stats"], ["stats"], bias=EPS)
                ts_("dve", gt[:, 2, :], gt[:, 2, :], stats[:, 6:7], stats[:, 7:8], ALU.subtract, ALU.mult, ["gt2", "stats"], ["gt2"])
                tt_("dve", gt[:, 2, :], gt[:, 2, :], bp[:, BP_LNG:BP_LNG + 512], ALU.mult, ["gt2", "bp"], ["gt2"])
                tt_("dve", vln[:, b, :], gt[:, 2, :], bp[:, BP_LNB:BP_LNB + 512], ALU.add, ["gt2", "bp"], ["vln%d" % b])
                step()
            if first:
                dump("vln", vln[:], [128, 4, 512], BF16, ["vln%d" % b for b in range(4)])
            for g in range(4):
                ps, pk = dense_ps()
                for b in range(4):
                    mm(ps[:, b * 128:(b + 1) * 128], vln[:, b, g * 128:(g + 1) * 128], wsT[:, g * 128:(g + 1) * 128], True, True,
                       ["vln%d" % b, "wsT"], [pk])
                y = gt[:, 1, :]
                tt_("dve", y.rearrange("p (b i) -> p b i", b=4), ps[:, 0:512].rearrange("p (b i) -> p b i", b=4),
                    bp[:, BP_BSB + g * 128:BP_BSB + (g + 1) * 128].unsqueeze(1).to_broadcast([128, 4, 128]), ALU.add,
                    [pk, "bp"], ["gt1"])
                tt_("dve", y, y, uT[:, g, :], ALU.mult, ["gt1", "uT%d" % g], ["gt1"])
                act(sqs[:, 0, 0:TT], y, AF.Square, ["gt1"], ["sqs0"])
                mm(psQ[:, 0:TT], ones_b[:], sqs[:, 0, 0:TT], True, True, ["ones_b", "sqs0"], ["psQ"])
                rsqrt(tmp_s[:, 0:TT], psQ[:, 0:TT], ["psQ"], ["tmp_s"], bias=EPS, scale=1.0 / 128)
                stt("dve", mixT[:, 4 + g, :], y, pp[:, PP_SGOG + g:PP_SGOG + g + 1], tmp_s[:, 0:TT], ALU.mult, ALU.mult,
                    ["gt1", "pp", "tmp_s"], ["mixT_sg"])
                step()
            if first:
                dump("ysg", mixT[:, 4:8, :], [128, 4, TT], BF16, ["mixT_sg"])
            dma("sync", scr_ysg[tt].rearrange("p (c t) -> p c t", c=4), mixT[:, 4:8, :], ["mixT_sg"], ["scr_ysg%d" % tt], "st_ysg")

        def delta1_gen(tt):
            yield from delta_tile(tt, 0)
            if tt == 0:
                dump("o1T", o1T[:], [128, 4, TT], BF16, ["o1T"])
            dma("sync", scr_o1[tt].rearrange("p (c t) -> p c t", c=4), o1T[:], ["o1T"], ["scr_o1%d" % tt], "st_o1")

        PSE0 = ["psE0"]
        PSE1 = ["psE1"]
        PSE = PSE0 + PSE1

        def delta_tile(tt, r):
            qb = tt % 2
            tt_("dve", t16[:], ab_tm2[:, qb], bp[:, BP_SGN:BP_SGN + 16].unsqueeze(1).to_broadcast([128, 4, 16]), ALU.mult,
                ["ab_tm%d" % qb, "bp"], ["t16"])
            tt_("dve", t16[:], t16[:], bp[:, BP_OFF:BP_OFF + 16].unsqueeze(1).to_broadcast([128, 4, 16]), ALU.add,
                ["t16", "bp"], ["t16"])
            act(t16[:], t16[:], AF.Exp, ["t16"], ["t16"])
            act(t16[:], t16[:], AF.Ln, ["t16"], ["t16"], bias=1.0)
            tt_("dve", G16[:], t16[:], mul16[:].unsqueeze(1).to_broadcast([128, 4, 16]), ALU.mult, ["t16", "mul16"], ["G16"])
            if tt == 0 and r == 0:
                dump("G16", G16[:], [128, 4, 16], F32, ["G16"])
            yield
            blocks = range(4) if r == 0 else range(3, -1, -1)
            for b in blocks:
                yield from delta_block(tt, b, r)

        def delta_block(tt, b, r):
            qb = tt % 2
            qkvT = qkvT2[qb]
            qk_keys = qkeys(qb)
            c0 = b * 128
            g4 = G16[:, b, r * 4:r * 4 + 4]
            lb4 = G16[:, b, 8 + r * 4:12 + r * 4]
            dbg0 = (tt == 0 and b == 0 and r == 0)
            mm(psS[:, 96:100], Mc[r], g4, True, True, ["cs", "G16"], ["psTS"])
            mm(psS[:, 100:104], BD, g4, True, True, ["cs", "G16"], ["psTS"])
            cp("dve", sc12[:, 0:4], psS[:, 96:100], ["psTS"], ["sc12"])
            tt_("dve", sc12[:, 4:8], psS[:, 100:104], sc12[:, 0:4], ALU.subtract, ["psTS", "sc12"], ["sc12"])
            cp("dve", sc12[:, 8:12], lb4, ["G16", "sc12"], ["sc12"])
            act(esc[:], sc12[:], AF.Exp, ["sc12"], ["esc"])
            ts_("dve", ngc[:], sc12[:, 0:4], -1.0, None, ALU.mult, None, ["sc12"], ["ngc"])
            ts_("dve", nbe[:], esc[:, 8:12], -1.0, None, ALU.mult, None, ["esc"], ["nbe"])
            yield
            tt_("dve", rhsP[:], Mc[r].unsqueeze(1).to_broadcast([128, 4, 128]), g4.unsqueeze(2).to_broadcast([128, 4, 128]),
                ALU.mult, ["cs", "G16"], ["rhsP"])
            psD3 = psD[:, 0:512].rearrange("p (h i) -> p h i", h=4)
            mm(psD[:, 0:512], ones_f[:], rhsP[:].rearrange("p h i -> p (h i)"), True, True, ["ones_f", "rhsP"], ["psD"])
            act(EGb[:], psD3, AF.Exp, ["psD"], ["EGb"])
            psG3 = psG[:, 0:512].rearrange("p (h i) -> p h i", h=4)
            mm(psG[:, 0:512], ones_f[:], rhsP[:].rearrange("p h i -> p (h i)"), True, False, ["ones_f", "rhsP"], ["psG"])
            mm(psG[:, 0:512], ident_b[:], negm_b[:, r, :], False, True, ["ident_b", "negm_b"], ["psG"])
            for h in range(4):
                act(decT[:, h, :], psG3[:, h, :], AF.Exp, ["psG", "ngc"], ["decT"], bias=ngc[:, h:h + 1])
            if dbg0:
                dump("decT", decT[:], [128, 4, 128], F32, ["decT"])
                dump("EGb", EGb[:], [128, 4, 128], F32, ["EGb"])
            yield
            stt("dve", qgT[:], qkvT[:, 0:4, c0:c0 + 128], QSCALE, EGb[:], ALU.mult, ALU.mult, qk_keys + ["EGb"], ["qgT"])
            for h in range(4):
                tr(psT[:, h, :], qkvT[:, 4 + h, c0:c0 + 128], qk_keys, ["psTS"])
            tt_("dve", kg[:], psT[:], esc[:, 0:4].unsqueeze(2).to_broadcast([128, 4, 128]), ALU.mult, ["psTS", "esc"], ["kg"])
            tt_("dve", ktl[:], psT[:], esc[:, 4:8].unsqueeze(2).to_broadcast([128, 4, 128]), ALU.mult, ["psTS", "esc"], ["ktl"])
            for h in range(4):
                tr(psT[:, h, :], qkvT[:, 8 + h, c0:c0 + 128], qk_keys, ["psTS"])
            cp("act", vtm[:], psT[:], ["psTS"], ["vtm"])
            yield
            psE4 = psE[:, :].rearrange("p (h two i) -> p h two i", h=4, two=2)
            for h in range(4):
                mm(psE4[:, h, 0, :], qkvT[:, 4 + h, c0:c0 + 128], qkvT[:, 4 + h, c0:c0 + 128], True, True, qk_keys, PSE)
                mm(psE4[:, h, 1, :], qkvT[:, 4 + h, c0:c0 + 128], qkvT[:, h, c0:c0 + 128], True, True, qk_keys, PSE)
            tt_("dve", Ebs[:], decT[:], esc[:, 8:12].unsqueeze(2).to_broadcast([128, 4, 128]), ALU.mult, ["decT", "esc"], ["Ebs"])
            tt_("dve", Ebs[:], Ebs[:], st01[:, r, :].rearrange("p (h i) -> p h i", h=4), ALU.mult, ["Ebs", "st01"], ["Ebs"])
            tt_("dve", Nm[:], psE4[:, :, 0, :], Ebs[:], ALU.mult, PSE + ["Ebs"], ["Nm"])
            tt_("dve", Pa[0][:], ident_b[:].unsqueeze(1).to_broadcast([128, 4, 128]), Nm[:], ALU.subtract, ["ident_b", "Nm"], ["Pa0"])
            stt("dve", attT[:], psE4[:, :, 1, :], QSCALE, decT[:], ALU.mult, ALU.mult, PSE + ["decT"], ["attT"])
            if dbg0:
                dump("Nm", Nm[:], [128, 4, 128], BF16, ["Nm"])
                dump("attT", attT[:], [128, 4, 128], BF16, ["attT"])
            yield
            for h in range(4):
                tr(psT[:, h, :], Nm[:, h, :], ["Nm"], ["psTS"])
            cp("act", NmT[:], psT[:], ["psTS"], ["NmT"])
            X, Xk, XT, XTk = Nm, "Nm", NmT, "NmT"
            pcur = 0
            psG3 = psG[:, 0:512].rearrange("p (h i) -> p h i", h=4)
            psE0_3 = psE[:, 0:512].rearrange("p (h i) -> p h i", h=4)
            psE1_3 = psE[:, 512:1024].rearrange("p (h i) -> p h i", h=4)
            for lvl in range(5):
                s = lvl % 2
                last = lvl == 4
                for h in range(4):
                    mm(psE0_3[:, h, :], X[:, h, :], XT[:, h, :], True, True, [Xk, XTk], PSE0)
                if not last:
                    for h in range(4):
                        mm(psG3[:, h, :], XT[:, h, :], X[:, h, :], True, True, [Xk, XTk], ["psG"])
                cp("act", XTa[s][:], psE0_3, PSE0, ["XTa%d" % s])
                if not last:
                    cp("dve", Xa[s][:], psG3, ["psG"], ["Xa%d" % s])
                yield
                X, Xk, XT, XTk = Xa[s], "Xa%d" % s, XTa[s], "XTa%d" % s
                pk0, pk1 = "Pa%d" % pcur, "Pa%d" % (1 - pcur)
                for h in range(4):
                    mm(psE1_3[:, h, :], XT[:, h, :], Pa[pcur][:, h, :], True, True, [XTk, pk0], PSE1)
                tt_("dve", Pa[1 - pcur][:], psE1_3, Pa[pcur][:], ALU.add, PSE1 + [pk0], [pk1])
                pcur = 1 - pcur
                yield
            Tt, Tk = Pa[pcur], "Pa%d" % pcur
            if dbg0:
                dump("Tt", Tt[:], [128, 4, 128], BF16, [Tk])
            for h in range(4):
                mm(psG3[:, h, :], Tt[:, h, :], vtm[:, h, :], True, True, [Tk, "vtm"], ["psG"])
            for h in range(4):
                mm(psE0_3[:, h, :], kg[:, h, :], Tt[:, h, :], True, True, [Tk, "kg"], PSE0)
            tt_("dve", ub[:], psG3, esc[:, 8:12].unsqueeze(2).to_broadcast([128, 4, 128]), ALU.mult, ["psG", "esc"], ["ub"])
            cp("act", wT[:], psE0_3, PSE0, ["wT"])
            yield
            chunks = (0, 1) if r == 0 else (1, 0)
            for ci in chunks:
                R = slice(64 * ci, 64 * ci + 64)
                lastcol = (64 * ci + 63) if r == 0 else (64 * ci)
                for h in range(4):
                    mm(psE0_3[:, h, :], wT[:, h, :], S_b[:, h, :], True, True, ["wT", "S_b%d" % h], ["psE0"])
                yield
                for h in range(4):
                    stt("dve", vnew[R, h, :], psE0_3[R, h, :], nbe[R, h:h + 1], ub[R, h, :], ALU.mult, ALU.add,
                        ["psE0", "nbe", "ub"], ["vnew%d" % h])
                for h in range(4):
                    mm(psD3[:, h, R], S_b[:, h, :], qgT[:, h, R], True, False, ["S_b%d" % h, "qgT"], ["psD"])
                    mm(psD3[:, h, R], vnew[R, h, :], attT[R, h, R], False, True, ["vnew%d" % h, "attT"], ["psD"])
                    mm(psE1_3[:, h, :], ktl[R, h, :], vnew[R, h, :], True, True, ["ktl", "vnew%d" % h], ["psE1"])
                for h in range(4):
                    stt("dve", S_f[:, h, :], S_f[:, h, :], EGb[:, h, lastcol:lastcol + 1], psE1_3[:, h, :], ALU.mult, ALU.add,
                        ["S_f%d" % h, "EGb", "psE1"], ["S_f%d" % h])
                    cp("act", S_b[:, h, :], S_f[:, h, :], ["S_f%d" % h], ["S_b%d" % h])
                yield
            if r == 0:
                cp("act", o1T[:, :, c0:c0 + 128], psD3, ["psD"], ["o1T"])
            else:
                tt_("dve", oacc[:, :, c0:c0 + 128], psD3, o1T[:, :, c0:c0 + 128], ALU.add, ["psD", "o1T"], ["oacc"])

        def sweep2_pre(tt):
            qb = tt % 2
            dma("sync", qkvT2[qb][:], scr_qkv[tt].rearrange("p (c t) -> p c t", c=12), ["scr_qkv%d" % tt], qkeys(qb), "ld_qkv%d" % qb)
            dma("sync", o1T[:], scr_o1[tt].rearrange("p (c t) -> p c t", c=4), ["scr_o1%d" % tt], ["o1T"], "ld_o1")
            dma("sync", ab_tm2[:, qb], scr_ab[tt].rearrange("p (b c) -> p b c", b=4), ["scr_ab%d" % tt], ["ab_tm%d" % qb], "ld_ab%d" % qb)
            return delta_tile(tt, 1)

        def sweep2_tile(l, tt, xsrc, xdst, lastlayer, nxt):
            t0 = tt * TT

            step = mk_step(nxt, 1.5)

            dma("sync", szT[:], scr_sz[tt].rearrange("p (c t) -> p c t", c=4), ["scr_sz%d" % tt], ["szT"], "ld_sz")
            dma("sync", mixT[:, 4:8, :], scr_ysg[tt].rearrange("p (c t) -> p c t", c=4), ["scr_ysg%d" % tt], ["mixT_sg"], "ld_ysg")
            load_x(xsrc, tt, False)
            if tt == 0:
                dump("oacc", oacc[:], [128, 4, TT], F32, ["oacc"])
            for h in range(4):
                act(sqs[:, 0, 0:TT], oacc[:, h, :], AF.Square, ["oacc"], ["sqs0"])
                mm(psQ[:, 0:TT], ones_b[:], sqs[:, 0, 0:TT], True, True, ["ones_b", "sqs0"], ["psQ"])
                rsqrt(tmp_s[:, 0:TT], psQ[:, 0:TT], ["psQ"], ["tmp_s"], bias=EPS, scale=1.0 / 128)
                stt("dve", tmp_s[:, 0:TT], oacc[:, h, :], pp[:, PP_DNNG:PP_DNNG + 1], tmp_s[:, 0:TT], ALU.mult, ALU.mult,
                    ["oacc", "pp", "tmp_s"], ["tmp_s"])
                tt_("dve", mixT[:, h, :], tmp_s[:, 0:TT], szT[:, h, :], ALU.mult, ["tmp_s", "szT"], ["mixT_dn"])
                step()
            if tt == 0:
                dump("mixT", mixT[:], [128, 8, TT], BF16, ["mixT_dn", "mixT_sg"])
            for g in range(2):
                wv, wk = w_next("out", l, g * 512)
                for mi in range(4):
                    m = g * 4 + mi
                    ps, pk = dense_ps()
                    for k in range(KD):
                        mm(ps[:, 0:TT], wv[:, k, mi * 128:(mi + 1) * 128], mixT[:, k, :], k == 0, k == KD - 1,
                           [wk, "mixT_dn", "mixT_sg"], [pk])
                    tt_("dve", xt[:, m, 0:TT], xt[:, m, 0:TT], ps[:, 0:TT], ALU.add, ["xt", pk], ["xt"])
                    step()
            if tt == 0:
                dump("xmid", xt[:, :, 0:TT], [128, KD, TT], F32, ["xt"])
            norm_stage(PP_GFFN, False, False)
            for g in range(6):
                n = 4 if g < 5 else 2
                wg, wgk = w_next("gate", l, g * 512)
                wu, wuk = w_next("up", l, g * 512)
                for mi in range(n):
                    m = g * 4 + mi
                    pg, pgk = dense_ps()
                    pu, puk = dense_ps()
                    for k in range(KD):
                        mm(pg[:, 0:TT], wg[:, k, mi * 128:(mi + 1) * 128], hT[:, k, 0:TT], k == 0, k == KD - 1, [wgk, "hT"], [pgk])
                    for k in range(KD):
                        mm(pu[:, 0:TT], wu[:, k, mi * 128:(mi + 1) * 128], hT[:, k, 0:TT], k == 0, k == KD - 1, [wuk, "hT"], [puk])
                    s = m % 2
                    act(cv[:, s, :], pg[:, 0:TT], AF.Silu, [pgk], ["cv%d" % s])
                    tt_("dve", hidT[:, m, :], cv[:, s, :], pu[:, 0:TT], ALU.mult, ["cv%d" % s, puk], ["hidT"])
                    step()
            for m in range(8):
                wv, wk = w_next("down", l, m * 128)
                ps, pk = dense_ps()
                for k in range(KF):
                    mm(ps[:, 0:TT], wv[:, k, :], hidT[:, k, :], k == 0, k == KF - 1, [wk, "hidT"], [pk])
                tt_("dve", xt[:, m, 0:TT], xt[:, m, 0:TT], ps[:, 0:TT], ALU.add, ["xt", pk], ["xt"])
                step()
            if tt == 0:
                dump("hid", hidT[:], [128, KF, TT], BF16, ["hidT"])
                dump("xfin", xt[:, :, 0:TT], [128, KD, TT], F32, ["xt"])
            if not lastlayer:
                dma("sync", xdst[0][:, :, t0:t0 + TT], xt[:, :, 0:TT], ["xt"], ["%s_%d" % (xdst[1], tt)], "st_x")
            else:
                for k in range(KD):
                    s = k % 2
                    act(sqs[:, s, 0:TT], xt[:, k, 0:TT], AF.Square, ["xt"], ["sqs%d" % s])
                    mm(psQ[:, 0:TT], ones_b[:], sqs[:, s, 0:TT], k == 0, k == KD - 1, ["ones_b", "sqs%d" % s], ["psQ"])
                rsqrt(rstd_b[:, 0:TT], psQ[:, 0:TT], ["psQ"], ["rstd_b"], bias=1024 * EPS)
                for k in range(KD):
                    s = k % 2
                    stt("dve", cv[:, s, :], xt[:, k, 0:TT], fg[:, k:k + 1], rstd_b[:, 0:TT], ALU.mult, ALU.mult,
                        ["xt", "fg", "rstd_b"], ["cv%d" % s])
                    dma("sync", out_d[:, k, t0:t0 + TT], cv[:, s, :], ["cv%d" % s], ["out%d" % k], "st_out%d" % s)

        for l in range(L):
            load_layer_params(l)
            xsrc = (xin, "xin") if l == 0 else (xs1, "xs1")
            xdst = (xs1, "xs1")
            lastlayer = l == L - 1
            for h in range(4):
                memset("dve", S_f[:, h, :], 0.0, ["S_f%d" % h])
                memset("dve", S_b[:, h, :], 0.0, ["S_b%d" % h])
            P.op("dve", lambda h: h.memset(dummy[:], 0.0), [], ["hidT"] + ["pbuf%d" % j for j in range(12)])
            proj_tile(l, 0, xsrc, None)
            for tt in range(NTT):
                g1 = delta1_gen(tt)
                if tt + 1 < NTT:
                    proj_tile(l, tt + 1, xsrc, g1)
                for _ in g1:
                    pass
            for h in range(4):
                memset("dve", S_f[:, h, :], 0.0, ["S_f%d" % h])
                memset("dve", S_b[:, h, :], 0.0, ["S_b%d" % h])
            P.op("dve", lambda h: h.memset(dummy[:], 0.0), [], ["hidT"] + ["pbuf%d" % j for j in range(12)])
            gen = sweep2_pre(NTT - 1)
            for _ in gen:
                pass
            for tt in range(NTT - 1, -1, -1):
                nxt = sweep2_pre(tt - 1) if tt > 0 else None
                sweep2_tile(l, tt, xsrc, xdst, lastlayer, nxt)
                if nxt is not None:
                    for _ in nxt:
                        pass
        P.op("sync", lambda h: h.nop(), ["out%d" % k for k in range(KD)] + ["dbg_" + n for n in dbg_outs], [])
        P.emit(st)
    return nc


def _consts():
    i = np.arange(128)
    same = (i[:, None] // 64) == (i[None, :] // 64)
    c = np.zeros((128, CS_N), np.float32)
    c[:, CS_ID:CS_ID + 128] = np.eye(128)
    c[:, CS_MC0:CS_MC0 + 128] = same & (i[:, None] <= i[None, :])
    c[:, CS_MC1:CS_MC1 + 128] = same & (i[:, None] >= i[None, :])
    c[:, CS_BD:CS_BD + 128] = same
    c[:, CS_NM0:CS_NM0 + 128] = np.where(same & (i[None, :] >= i[:, None]), 0.0, -30000.0)
    c[:, CS_NM1:CS_NM1 + 128] = np.where(same & (i[None, :] <= i[:, None]), 0.0, -30000.0)
    c[:, CS_ST0:CS_ST0 + 128] = same & (i[None, :] > i[:, None])
    c[:, CS_ST1:CS_ST1 + 128] = same & (i[None, :] < i[:, None])
    return c


def _host_params(inp, L):
    f = np.float32
    pp = np.zeros((128, L, PP_N), f)
    bp = np.zeros((128, L, BP_N), f)
    sgwT = np.zeros((L, 128, 512), f)
    for l in range(L):
        pp[:, l, PP_GMIX:PP_GMIX + 8] = np.asarray(inp["mix_norm_g"][l]).reshape(8, 128).T
        pp[:, l, PP_GFFN:PP_GFFN + 8] = np.asarray(inp["ffn_norm_g"][l]).reshape(8, 128).T
        cw = np.asarray(inp["conv_w"][l])
        pp[:, l, PP_CONV:PP_CONV + 36] = cw.reshape(3, 12, 128).transpose(2, 1, 0).reshape(128, 36)
        pp[:, l, PP_DNNG] = np.asarray(inp["dn_norm_g"][l])
        pp[:, l, PP_SGOG:PP_SGOG + 4] = np.asarray(inp["sg_out_g"][l]).reshape(4, 128).T
        bp[:, l, BP_ALOG:BP_ALOG + 8] = np.asarray(inp["dn_a_log"][l]).reshape(8)[None, :]
        bp[:, l, BP_OFF:BP_OFF + 8] = np.asarray(inp["dn_dt_bias"][l]).reshape(8)[None, :]
        bp[:, l, BP_SGN:BP_SGN + 8] = 1.0
        bp[:, l, BP_SGN + 8:BP_SGN + 16] = -1.0
        bp[:, l, BP_LNG:BP_LNG + 512] = np.asarray(inp["sg_ln_g"][l])[None, :]
        bp[:, l, BP_LNB:BP_LNB + 512] = np.asarray(inp["sg_ln_b"][l])[None, :]
        bp[:, l, BP_BSB:BP_BSB + 512] = np.asarray(inp["sg_b"][l]).reshape(512)[None, :]
        sgwT[l] = np.asarray(inp["sg_w"][l]).transpose(2, 0, 1).reshape(128, 512)
    fgv = np.ascontiguousarray(np.asarray(inp["final_norm_g"]).reshape(8, 128).T).astype(f)
    return pp, bp, sgwT, fgv


_NC_CACHE = {}


def run(inp, dbg=None, trace=False):
    x = np.asarray(inp["x"])
    B, S, _ = x.shape
    L = np.asarray(inp["w_in"]).shape[0]
    key = (S, L, tuple(sorted(dbg or ())))
    if key not in _NC_CACHE:
        _NC_CACHE[key] = build(S, L, dbg)
    nc = _NC_CACHE[key]
    pp, bp, sgwT, fgv = _host_params(inp, L)
    cs = _consts()
    shared = {
        "w_in": np.ascontiguousarray(inp["w_in"], np.float32), "w_out": np.ascontiguousarray(inp["w_out"], np.float32),
        "w_gate": np.ascontiguousarray(inp["w_gate"], np.float32), "w_up": np.ascontiguousarray(inp["w_up"], np.float32),
        "w_down": np.ascontiguousarray(inp["w_down"], np.float32),
        "pp": pp, "fg": fgv, "bp": bp, "sgwT": sgwT, "consts": cs,
    }
    in_maps = []
    for b in range(B):
        xi = np.zeros((128, KD, S + 1), np.float32)
        xi[:, :, 0:S] = x[b].reshape(S, KD, 128).transpose(2, 1, 0)
        m = dict(shared)
        m["xin"] = xi
        in_maps.append(m)
    res = run_bass_kernel_spmd(nc, in_maps, core_ids=list(range(B)), **({"trace": True} if trace else {}))
    out = np.zeros((B, S, D), np.float32)
    for b in range(B):
        out[b] = res.results[b]["out"].transpose(2, 1, 0).reshape(S, D)
    return out, res


def kernel(**inputs):
    out, _ = run(inputs)
    return out
```

```python
import math
from contextlib import ExitStack

import numpy as np

import concourse.bass as bass
import concourse.mybir as mybir
from concourse.bass_utils import run_bass_kernel_spmd

F32 = mybir.dt.float32
BF16 = mybir.dt.bfloat16
AF = mybir.ActivationFunctionType
ALU = mybir.AluOpType

D = 1024
KD = 8
PW = 3088
FF = 2816
KF = 22
TT = 512
EPS = 1e-6
C_Q, C_K, C_V, C_Z, C_AB, C_U, C_VS = 0, 512, 1024, 1536, 2048, 2064, 2576
QSCALE = 128.0 ** -0.5
GELU_C = 1.5957691216057308

PP_GMIX, PP_GFFN, PP_CONV, PP_DNNG, PP_SGOG, PP_N = 0, 8, 16, 52, 53, 57
BP_ALOG, BP_OFF, BP_SGN, BP_LNG, BP_LNB, BP_BSB, BP_N = 0, 16, 32, 48, 560, 1072, 1584
CS_ID, CS_MC0, CS_MC1, CS_BD, CS_NM0, CS_NM1, CS_ST0, CS_ST1, CS_N = 0, 128, 256, 384, 512, 640, 768, 896, 1024


class _I:
    __slots__ = ("fn", "deps", "dma", "dkey", "val", "needs_inc")

    def __init__(self, fn, deps, dma, dkey):
        self.fn = fn
        self.deps = deps
        self.dma = dma
        self.dkey = dkey
        self.val = 0
        self.needs_inc = False


class Prog:
    ENGS = ("sync", "act", "dve", "pool", "pe")

    def __init__(self, nc):
        self.nc = nc
        self.ins = {e: [] for e in self.ENGS}
        self.last_w = {}
        self.readers = {}
        self.dcount = {}

    def op(self, eng, fn, reads=(), writes=(), dkey=None):
        deps = set()
        for k in reads:
            lw = self.last_w.get(k)
            if lw is not None:
                deps.add(lw)
        for k in writes:
            lw = self.last_w.get(k)
            if lw is not None:
                deps.add(lw)
            for r in self.readers.get(k, ()):
                deps.add(r)
        idx = len(self.ins[eng])
        dma = dkey is not None
        if eng == "pe" and not dma:
            deps = {d for d in deps if d[0] != "pe"}
        it = _I(fn, deps, dma, dkey)
        if dma:
            c = self.dcount.get(dkey, 0) + 16
            self.dcount[dkey] = c
            it.val = c
        self.ins[eng].append(it)
        me = (eng, idx)
        for k in reads:
            lst = self.readers.setdefault(k, [])
            if not dma:
                lst[:] = [r for r in lst if not (r[0] == eng and not self.ins[r[0]][r[1]].dma)]
            lst.append(me)
        for k in writes:
            self.last_w[k] = me
            self.readers[k] = []
        return me

    def emit(self, stack):
        nc = self.nc
        for e in self.ENGS:
            for it in self.ins[e]:
                for (de, di) in it.deps:
                    self.ins[de][di].needs_inc = True
        csem = {}
        for e in self.ENGS:
            csem[e] = stack.enter_context(nc.semaphore("c_" + e))
            n = 0
            for it in self.ins[e]:
                if not it.dma and it.needs_inc:
                    n += 1
                    it.val = n
        dsem = {}
        for k in self.dcount:
            dsem[k] = stack.enter_context(nc.semaphore("d_" + str(k)))
        block = stack.enter_context(nc.Block())
        ins = self.ins

        def run(e, h):
            waited = {}
            for it in ins[e]:
                need = {}
                for (de, di) in it.deps:
                    d = ins[de][di]
                    s = dsem[d.dkey] if d.dma else csem[de]
                    key = id(s)
                    if waited.get(key, 0) >= d.val:
                        continue
                    if key not in need or need[key][1] < d.val:
                        need[key] = (s, d.val)
                for key, (s, v) in need.items():
                    h.wait_ge(s, v)
                    waited[key] = v
                r = it.fn(h)
                if it.dma:
                    r.then_inc(dsem[it.dkey], 16)
                elif it.needs_inc:
                    r.then_inc(csem[e], 1)

        @block.sync
        def _(h):
            run("sync", h)

        @block.scalar
        def _(h):
            run("act", h)

        @block.vector
        def _(h):
            run("dve", h)

        @block.gpsimd
        def _(h):
            run("pool", h)

        @block.tensor
        def _(h):
            run("pe", h)


def build(NT, L, dbg=None):
    NTT = NT // TT
    dbg = dbg or set()
    nc = bass.Bass("TRN2", target_bir_lowering=False)

    def din(name, shape, dt=F32):
        return nc.dram_tensor(name, list(shape), dt, kind="ExternalInput").ap()

    xin = din("xin", [128, KD, NT + 1])
    w_in = din("w_in", [L, D, PW])
    w_out = din("w_out", [L, D, D])
    w_gate = din("w_gate", [L, D, FF])
    w_up = din("w_up", [L, D, FF])
    w_down = din("w_down", [L, FF, D])
    pp_d = din("pp", [128, L, PP_N])
    fg_d = din("fg", [128, KD])
    bp_d = din("bp", [128, L, BP_N])
    sgw_d = din("sgwT", [L, 128, 512])
    cs_d = din("consts", [128, CS_N])
    out_d = nc.dram_tensor("out", [128, KD, NT], F32, kind="ExternalOutput").ap()

    xs1 = nc.dram_tensor("xs1", [128, KD, NT + 1], F32, kind="Internal").ap()
    scr_qkv = nc.dram_tensor("scr_qkv", [NTT, 128, 12 * TT], BF16, kind="Internal").ap()
    scr_sz = nc.dram_tensor("scr_sz", [NTT, 128, 4 * TT], BF16, kind="Internal").ap()
    scr_ysg = nc.dram_tensor("scr_ysg", [NTT, 128, 4 * TT], BF16, kind="Internal").ap()
    scr_o1 = nc.dram_tensor("scr_o1", [NTT, 128, 4 * TT], BF16, kind="Internal").ap()
    scr_ab = nc.dram_tensor("scr_ab", [NTT, 128, 64], F32, kind="Internal").ap()

    dbg_outs = {}

    with ExitStack() as st:
        def sb(name, shape, dt=F32):
            return st.enter_context(nc.sbuf_tensor("s_" + name, list(shape), dt))

        def pst(name, shape, dt=F32):
            return st.enter_context(nc.psum_tensor("p_" + name, list(shape), dt))

        cs = sb("cs", [128, CS_N])
        ident_b = sb("ident_b", [128, 128], BF16)
        ones_b = sb("ones_b", [128, 128], BF16)
        ones_f = sb("ones_f", [128, 128])
        negm_b = sb("negm_b", [128, 2, 512], BF16)
        st01 = sb("st01", [128, 2, 512])
        zero_f = sb("zero_f", [128, 8])
        pp = sb("pp", [128, PP_N])
        g32 = sb("g32", [128, 16])
        fg = sb("fg", [128, KD])
        bp = sb("bp", [128, BP_N])
        mul16 = sb("mul16", [128, 16])
        wsT = sb("wsT", [128, 512], BF16)
        wab = sb("wab", [128, KD, 16], BF16)
        xt = sb("xt", [128, KD, TT + 1])
        hT = sb("hT", [128, KD, TT + 1], BF16)
        sqs = sb("sqs", [128, 2, TT + 1], BF16)
        rstd_b = sb("rstd_b", [128, TT + 1])
        rstd_tm = sb("rstd_tm", [128, 4])
        tmp_s = sb("tmp_s", [128, TT + 8])
        NWS = 4
        wsl = [sb("wsl%d" % i, [128, 4096], BF16) for i in range(NWS)]
        arena = sb("arena", [128, 12 * (TT + 2)])
        pbuf = arena[:, :].rearrange("p (c t) -> p c t", c=12)
        hidT = arena[:, 0:KF * TT // 2].bitcast(BF16).rearrange("p (k t) -> p k t", k=KF)
        dummy = sb("dummy", [128, 2])
        carry = sb("carry", [128, 12])
        cv = sb("cv", [128, 2, TT])
        uT = sb("uT", [128, 4, TT])
        gt = sb("gt", [128, 3, TT])
        vln = sb("vln", [128, 4, 512], BF16)
        qkvT2 = [sb("qkvT_%d" % i, [128, 12, TT], BF16) for i in range(2)]
        szT = sb("szT", [128, 4, TT], BF16)
        mixT = sb("mixT", [128, 8, TT], BF16)
        o1T = sb("o1T", [128, 4, TT], BF16)
        oacc = sb("oacc", [128, 4, TT])
        ab_tm2 = sb("ab_tm", [128, 2, 4, 16])
        stats = sb("stats", [128, 8])
        stats2 = sb("stats2", [128, 2])
        sqh = sb("sqh", [128, 8], BF16)
        t16 = sb("t16", [128, 4, 16])
        G16 = sb("G16", [128, 4, 16])
        sc12 = sb("sc12", [128, 12])
        esc = sb("esc", [128, 12])
        ngc = sb("ngc", [128, 4])
        nbe = sb("nbe", [128, 4])
        rhsP = sb("rhsP", [128, 4, 128])
        decT = sb("decT", [128, 4, 128])
        EGb = sb("EGb", [128, 4, 128])
        Ebs = sb("Ebs", [128, 4, 128])
        qgT = sb("qgT", [128, 4, 128], BF16)
        kg = sb("kg", [128, 4, 128], BF16)
        ktl = sb("ktl", [128, 4, 128], BF16)
        vtm = sb("vtm", [128, 4, 128], BF16)
        Nm = sb("Nm", [128, 4, 128], BF16)
        NmT = sb("NmT", [128, 4, 128], BF16)
        Xa = [sb("Xa%d" % i, [128, 4, 128], BF16) for i in range(2)]
        XTa = [sb("XTa%d" % i, [128, 4, 128], BF16) for i in range(2)]
        Pa = [sb("Pa%d" % i, [128, 4, 128], BF16) for i in range(2)]
        attT = sb("attT", [128, 4, 128], BF16)
        ub = sb("ub", [128, 4, 128])
        wT = sb("wT", [128, 4, 128], BF16)
        vnew = sb("vnew", [128, 4, 128], BF16)
        S_f = sb("S_f", [128, 4, 128])
        S_b = sb("S_b", [128, 4, 128], BF16)

        psA = pst("psA", [128, 512])
        psB = pst("psB", [128, 512])
        psQ = pst("psQ", [128, 512])
        psD = pst("psD", [128, 512])
        psE = pst("psE", [128, 1024])
        psG = pst("psG", [128, 512])
        psTS = pst("psTS", [128, 512])
        psT = psTS[:, 0:256].bitcast(BF16).rearrange("p (h i) -> p h i", h=4)
        psS = psTS[:, 384:512]

        P = Prog(nc)

        def mm(out, lhsT, rhs, start, stop, r, w):
            P.op("pe", lambda h: h.matmul(out, lhsT=lhsT, rhs=rhs, start=start, stop=stop), r, w)

        def tr(out, in_, r, w):
            P.op("pe", lambda h: h.transpose(out, in_, ident_b[:]), list(r) + ["ident_b"], w)

        def tt_(eng, out, in0, in1, op, r, w):
            P.op(eng, lambda h: h.tensor_tensor(out=out, in0=in0, in1=in1, op=op), r, w)

        def ts_(eng, out, in0, s1, s2, op0, op1, r, w):
            if s2 is None:
                P.op(eng, lambda h: h.tensor_scalar(out=out, in0=in0, scalar1=s1, scalar2=None, op0=op0), r, w)
            else:
                P.op(eng, lambda h: h.tensor_scalar(out=out, in0=in0, scalar1=s1, scalar2=s2, op0=op0, op1=op1), r, w)

        def stt(eng, out, in0, scalar, in1, op0, op1, r, w):
            P.op(eng, lambda h: h.scalar_tensor_tensor(out=out, in0=in0, scalar=scalar, in1=in1, op0=op0, op1=op1), r, w)

        def act(out, in_, func, r, w, bias=0.0, scale=1.0):
            P.op("act", lambda h: h.activation(out=out, in_=in_, func=func, bias=bias, scale=scale), r, w)

        def cp(eng, out, in_, r, w):
            if eng == "act":
                P.op("act", lambda h: h.copy(out=out, in_=in_), r, w)
            else:
                P.op(eng, lambda h: h.tensor_copy(out=out, in_=in_), r, w)

        def recip(out, in_, r, w):
            P.op("dve", lambda h: h.reciprocal(out=out, in_=in_), r, w)

        def dma(eng, out, in_, r, w, dkey, slow=False):
            if slow:
                P.op(eng, lambda h: h.dma_start(out=out, in_=in_, allow_slow_non_contiguous=True), r, w, dkey=dkey)
            else:
                P.op(eng, lambda h: h.dma_start(out=out, in_=in_), r, w, dkey=dkey)

        def memset(eng, ap, val, w):
            P.op(eng, lambda h: h.memset(ap, val), (), w)

        def dump(name, ap, shape, dt, rkeys):
            if name not in dbg:
                return
            o = nc.dram_tensor("dbg_" + name, list(shape), dt, kind="ExternalOutput").ap()
            dbg_outs[name] = o
            dma("sync", o, ap, rkeys, ["dbg_" + name], "dbg_" + name)

        def rsqrt(out, in_, r, w, bias, scale=1.0):
            act(out, in_, AF.Ln, r, w, bias=bias, scale=scale)
            act(out, out, AF.Exp, w, w, scale=-0.5)

        plan = []
        for l in range(L):
            for tt in range(NTT):
                for nm, c0 in (("q", C_Q), ("k", C_K), ("v", C_V), ("z", C_Z), ("u", C_U), ("vs", C_VS)):
                    plan.append(("in", l, c0, 512))
            for tt in range(NTT):
                for g in range(2):
                    plan.append(("out", l, g * 512, 512))
                for g in range(6):
                    n = 512 if g < 5 else 256
                    plan.append(("gate", l, g * 512, n))
                    plan.append(("up", l, g * 512, n))
                for m in range(8):
                    plan.append(("down", l, m * 128, 128))
        wstate = {"issued": 0, "used": 0}

        def w_issue(i):
            kind, l, c0, n = plan[i]
            slot = i % NWS
            key = "wsl%d" % slot
            if kind == "down":
                src = w_down[l].rearrange("(k p) c -> p k c", p=128)[:, :, c0:c0 + n]
                dst = wsl[slot][:, 0:KF * n].rearrange("p (k c) -> p k c", k=KF)
            else:
                wsrc = {"in": w_in, "out": w_out, "gate": w_gate, "up": w_up}[kind]
                src = wsrc[l].rearrange("(k p) c -> p k c", p=128)[:, :, c0:c0 + n]
                dst = wsl[slot][:, 0:KD * n].rearrange("p (k c) -> p k c", k=KD)
            dma("pool", dst, src, [], [key], key)

        def w_next(kind, l, c0):
            i = wstate["used"]
            assert plan[i][0] == kind and plan[i][1] == l and plan[i][2] == c0, (plan[i], kind, l, c0)
            while wstate["issued"] < min(len(plan), i + NWS - 1):
                w_issue(wstate["issued"])
                wstate["issued"] += 1
            wstate["used"] += 1
            slot = i % NWS
            n = plan[i][3]
            k = KF if kind == "down" else KD
            return wsl[slot][:, 0:k * n].rearrange("p (k c) -> p k c", k=k), "wsl%d" % slot

        dma("sync", cs[:], cs_d[:], [], ["cs"], "cs")
        cp("dve", ident_b[:], cs[:, CS_ID:CS_ID + 128], ["cs"], ["ident_b"])
        memset("dve", ones_b[:], 1.0, ["ones_b"])
        memset("dve", ones_f[:], 1.0, ["ones_f"])
        memset("dve", zero_f[:], 0.0, ["zero_f"])
        for r_ in range(2):
            c_nm = CS_NM0 if r_ == 0 else CS_NM1
            c_st = CS_ST0 if r_ == 0 else CS_ST1
            cp("dve", negm_b[:, r_, :].rearrange("p (h i) -> p h i", h=4),
               cs[:, c_nm:c_nm + 128].unsqueeze(1).to_broadcast([128, 4, 128]), ["cs"], ["negm_b"])
            cp("dve", st01[:, r_, :].rearrange("p (h i) -> p h i", h=4),
               cs[:, c_st:c_st + 128].unsqueeze(1).to_broadcast([128, 4, 128]), ["cs"], ["st01"])
        dma("sync", fg[:], fg_d[:], [], ["fg"], "fg")
        ts_("dve", fg[:], fg[:], 32.0, None, ALU.mult, None, ["fg"], ["fg"])
        dma("sync", xs1[:, :, NT:NT + 1], zero_f[:, :].unsqueeze(2), ["zero_f"], ["xs1_%d" % NTT], "xs1h", slow=True)
        Mc = [cs[:, CS_MC0:CS_MC0 + 128], cs[:, CS_MC1:CS_MC1 + 128]]
        BD = cs[:, CS_BD:CS_BD + 128]

        def load_layer_params(l):
            dma("sync", pp[:], pp_d[:, l, :], [], ["pp"], "pp")
            ts_("dve", g32[:], pp[:, 0:16], 32.0, None, ALU.mult, None, ["pp"], ["g32"])
            dma("sync", bp[:], bp_d[:, l, :], [], ["bp"], "bp")
            act(mul16[:], bp[:, BP_ALOG:BP_ALOG + 16], AF.Exp, ["bp"], ["mul16"])
            ts_("dve", mul16[:], mul16[:], -1.0, None, ALU.mult, None, ["mul16"], ["mul16"])
            dma("pool", wsT[:], sgw_d[l], [], ["wsT"], "wsT")
            dma("pool", wab[:], w_in[l].rearrange("(k p) c -> p k c", p=128)[:, :, C_AB:C_AB + 16], [], ["wab"], "wab")

        def load_x(src, tt, halo):
            t0 = tt * TT
            n = TT + 1 if halo else TT
            rk = ["%s_%d" % (src[1], tt)] + (["%s_%d" % (src[1], tt + 1)] if halo else [])
            dma("sync", xt[:, :, 0:n], src[0][:, :, t0:t0 + n], rk, ["xt"], "xt")

        dense_ctr = [0]

        wide_ctr = [0]

        def dense_ps(wide=False):
            if wide:
                i = wide_ctr[0]
                wide_ctr[0] += 1
                return [(psA, "psA"), (psB, "psB"), (psD, "psD"), (psG, "psG"),
                        (psE[:, 0:512], "psE0"), (psE[:, 512:1024], "psE1")][i % 6]
            i = dense_ctr[0]
            dense_ctr[0] += 1
            return (psA, "psA") if i % 2 == 0 else (psB, "psB")

        def norm_stage(gcol, halo, tokmajor):
            n = TT + 1 if halo else TT
            for k in range(KD):
                s = k % 2
                act(sqs[:, s, 0:TT], xt[:, k, 0:TT], AF.Square, ["xt"], ["sqs%d" % s])
                mm(psQ[:, 0:TT], ones_b[:], sqs[:, s, 0:TT], k == 0, k == KD - 1, ["ones_b", "sqs%d" % s], ["psQ"])
            rsqrt(rstd_b[:, 0:TT], psQ[:, 0:TT], ["psQ"], ["rstd_b"], bias=1024 * EPS)
            rk = ["rstd_b"]
            if halo:
                act(sqh[:, 0:8], xt[:, :, TT], AF.Square, ["xt"], ["sqh"])
                psh, phk = dense_ps()
                mm(psh[:, 0:8], ones_b[:], sqh[:, 0:8], True, True, ["ones_b", "sqh"], [phk])
                P.op("dve", lambda h: h.reduce_sum(out=stats2[:, 0:1], in_=psh[:, 0:8], axis=mybir.AxisListType.X),
                     [phk], ["stats2"])
                rsqrt(rstd_b[:, TT:TT + 1], stats2[:, 0:1], ["stats2"], ["rstd_bh"], bias=1024 * EPS)
                rk = ["rstd_b", "rstd_bh"]
            for k in range(KD):
                stt("dve", hT[:, k, 0:n], xt[:, k, 0:n], g32[:, gcol + k:gcol + k + 1], rstd_b[:, 0:n], ALU.mult, ALU.mult,
                    ["xt", "g32"] + rk, ["hT"])

        def gelu(x_ap, out_ap, xk, ok, tslot):
            a = gt[:, tslot, :]
            ak = "gt%d" % tslot
            tt_("dve", a, x_ap, x_ap, ALU.mult, xk, [ak])
            ts_("dve", a, a, 0.044715, 1.0, ALU.mult, ALU.add, [ak], [ak])
            tt_("dve", a, a, x_ap, ALU.mult, [ak] + list(xk), [ak])
            act(a, a, AF.Sigmoid, [ak], [ak], scale=GELU_C)
            tt_("dve", out_ap, x_ap, a, ALU.mult, [ak] + list(xk), ok)

        def qkeys(qb):
            return ["qkvT%d_%d" % (qb, j) for j in range(12)]

        def mk_step(gen, ratio):
            acc = [0.0]

            def step():
                if gen is None:
                    return
                acc[0] += ratio
                while acc[0] >= 1.0:
                    acc[0] -= 1.0
                    try:
                        next(gen)
                    except StopIteration:
                        return
            return step

        def proj_tile(l, tt, xsrc, gen):
            step = mk_step(gen, 0.0)
            first = tt == 0
            qb = tt % 2
            qkvT = qkvT2[qb]
            ab_tm = ab_tm2[:, qb]
            load_x(xsrc, tt, True)
            norm_stage(PP_GMIX, True, True)
            if first:
                dump("hT", hT[:], [128, KD, TT + 1], BF16, ["hT"])
                dump("rstd_b", rstd_b[:], [128, TT + 1], F32, ["rstd_b", "rstd_bh"])
                memset("dve", carry[:], 0.0, ["carry0", "carry1", "carry2"])
            step()
            for gi, c0 in enumerate((C_Q, C_K, C_V)):
                wv, wk = w_next("in", l, c0)
                for mi in range(4):
                    j = gi * 4 + mi
                    ps, pk = dense_ps(True)
                    for k in range(KD):
                        mm(ps[:, 0:TT], wv[:, k, mi * 128:(mi + 1) * 128], hT[:, k, 0:TT], k == 0, k == KD - 1, [wk, "hT"], [pk])
                    cp("act", pbuf[:, j, 1:TT + 1], ps[:, 0:TT], [pk], ["pbuf%d" % j])
                psh, phk = dense_ps(True)
                for mi in range(4):
                    for k in range(KD):
                        mm(psh[:, mi:mi + 1], wv[:, k, mi * 128:(mi + 1) * 128], hT[:, k, TT:TT + 1], k == 0, k == KD - 1,
                           [wk, "hT"], [phk])
                jk = ["pbuf%d" % j for j in range(gi * 4, gi * 4 + 4)]
                cp("dve", pbuf[:, gi * 4:gi * 4 + 4, 0], carry[:, gi * 4:gi * 4 + 4], ["carry%d" % gi], jk)
                cp("dve", pbuf[:, gi * 4:gi * 4 + 4, TT + 1], psh[:, 0:4], [phk], jk)
                cp("dve", carry[:, gi * 4:gi * 4 + 4], pbuf[:, gi * 4:gi * 4 + 4, TT], jk, ["carry%d" % gi])
                for mi in range(4):
                    j = gi * 4 + mi
                    s = j % 2
                    ck = "cv%d" % s
                    pk_ = "pbuf%d" % j
                    act(cv[:, s, :], pbuf[:, j, 1:TT + 1], AF.Copy, [pk_, "pp"], [ck], scale=pp[:, PP_CONV + 3 * j + 1:PP_CONV + 3 * j + 2])
                    stt("dve", cv[:, s, :], pbuf[:, j, 0:TT], pp[:, PP_CONV + 3 * j:PP_CONV + 3 * j + 1], cv[:, s, :], ALU.mult, ALU.add,
                        [pk_, "pp", ck], [ck])
                    stt("dve", cv[:, s, :], pbuf[:, j, 2:TT + 2], pp[:, PP_CONV + 3 * j + 2:PP_CONV + 3 * j + 3], cv[:, s, :], ALU.mult, ALU.add,
                        [pk_, "pp", ck], [ck])
                    if j >= 8:
                        act(qkvT[:, j, :], cv[:, s, :], AF.Silu, [ck], ["qkvT%d_%d" % (qb, j)])
                    else:
                        act(cv[:, s, :], cv[:, s, :], AF.Silu, [ck], [ck])
                        act(sqs[:, s, 0:TT], cv[:, s, :], AF.Square, [ck], ["sqs%d" % s])
                        mm(psQ[:, 0:TT], ones_b[:], sqs[:, s, 0:TT], True, True, ["ones_b", "sqs%d" % s], ["psQ"])
                        rsqrt(tmp_s[:, 0:TT], psQ[:, 0:TT], ["psQ"], ["tmp_s"], bias=EPS)
                        tt_("dve", qkvT[:, j, :], cv[:, s, :], tmp_s[:, 0:TT], ALU.mult, [ck, "tmp_s"], ["qkvT%d_%d" % (qb, j)])
                    step()
            if first:
                dump("pbuf", pbuf[:, :, :], [128, 12, TT + 2], F32, ["pbuf%d" % j for j in range(12)])
                dump("qkvT", qkvT[:], [128, 12, TT], BF16, qkeys(qb))
            dma("sync", scr_qkv[tt].rearrange("p (c t) -> p c t", c=12), qkvT[:], qkeys(qb), ["scr_qkv%d" % tt], "st_qkv%d" % qb)
            wv, wk = w_next("in", l, C_Z)
            for mi in range(4):
                ps, pk = dense_ps(True)
                for k in range(KD):
                    mm(ps[:, 0:TT], wv[:, k, mi * 128:(mi + 1) * 128], hT[:, k, 0:TT], k == 0, k == KD - 1, [wk, "hT"], [pk])
                act(szT[:, mi, :], ps[:, 0:TT], AF.Silu, [pk], ["szT"])
                step()
            dma("sync", scr_sz[tt].rearrange("p (c t) -> p c t", c=4), szT[:], ["szT"], ["scr_sz%d" % tt], "st_sz")
            wv, wk = w_next("in", l, C_U)
            for mi in range(4):
                ps, pk = dense_ps(True)
                for k in range(KD):
                    mm(ps[:, 0:TT], wv[:, k, mi * 128:(mi + 1) * 128], hT[:, k, 0:TT], k == 0, k == KD - 1, [wk, "hT"], [pk])
                cp("act", gt[:, 1, :], ps[:, 0:TT], [pk], ["gt1"])
                gelu(gt[:, 1, :], uT[:, mi, :], ["gt1"], ["uT%d" % mi], 0)
                step()
            psh, phk = dense_ps(True)
            for b in range(4):
                for k in range(KD):
                    mm(psh[:, 16 * b:16 * b + 16], hT[:, k, b * 128:(b + 1) * 128], wab[:, k, :], k == 0, k == KD - 1,
                       ["wab", "hT"], [phk])
            cp("dve", ab_tm, psh[:, 0:64].rearrange("p (b c) -> p b c", b=4), [phk], ["ab_tm%d" % qb])
            dma("sync", scr_ab[tt].rearrange("p (b c) -> p b c", b=4), ab_tm, ["ab_tm%d" % qb], ["scr_ab%d" % tt], "st_ab%d" % qb)
            if first:
                dump("ab_tm", ab_tm, [128, 4, 16], F32, ["ab_tm%d" % qb])
            wv, wk = w_next("in", l, C_VS)
            for b in range(4):
                ps, pk = dense_ps(True)
                for k in range(KD):
                    mm(ps[:, 0:512], hT[:, k, b * 128:(b + 1) * 128], wv[:, k, :], k == 0, k == KD - 1, [wk, "hT"], [pk])
                cp("act", gt[:, 1, :], ps[:, 0:512], [pk], ["gt1"])
                gelu(gt[:, 1, :], gt[:, 2, :], ["gt1"], ["gt2"], 0)
                P.op("dve", lambda h: h.bn_stats(out=stats[:, 0:6], in_=gt[:, 2, :]), ["gt2"], ["stats"])
                P.op("dve", lambda h: h.bn_aggr(out=stats[:, 6:8], in_=stats[:, 0:6]), ["stats"], ["stats"])
                rsqrt(stats[:, 7:8], stats[:, 7:8], ["stats"], ["stats"], bias=EPS)
                ts_("dve", gt[:, 2, :], gt[:, 2, :], stats[:, 6:7], stats[:, 7:8], ALU.subtract, ALU.mult, ["gt2", "stats"], ["gt2"])
                tt_("dve", gt[:, 2, :], gt[:, 2, :], bp[:, BP_LNG:BP_LNG + 512], ALU.mult, ["gt2", "bp"], ["gt2"])
                tt_("dve", vln[:, b, :], gt[:, 2, :], bp[:, BP_LNB:BP_LNB + 512], ALU.add, ["gt2", "bp"], ["vln%d" % b])
                step()
            if first:
                dump("vln", vln[:], [128, 4, 512], BF16, ["vln%d" % b for b in range(4)])
            for g in range(4):
                ps, pk = dense_ps(True)
                for b in range(4):
                    mm(ps[:, b * 128:(b + 1) * 128], vln[:, b, g * 128:(g + 1) * 128], wsT[:, g * 128:(g + 1) * 128], True, True,
                       ["vln%d" % b, "wsT"], [pk])
                y = gt[:, 1, :]
                tt_("dve", y.rearrange("p (b i) -> p b i", b=4), ps[:, 0:512].rearrange("p (b i) -> p b i", b=4),
                    bp[:, BP_BSB + g * 128:BP_BSB + (g + 1) * 128].unsqueeze(1).to_broadcast([128, 4, 128]), ALU.add,
                    [pk, "bp"], ["gt1"])
                tt_("dve", y, y, uT[:, g, :], ALU.mult, ["gt1", "uT%d" % g], ["gt1"])
                act(sqs[:, 0, 0:TT], y, AF.Square, ["gt1"], ["sqs0"])
                mm(psQ[:, 0:TT], ones_b[:], sqs[:, 0, 0:TT], True, True, ["ones_b", "sqs0"], ["psQ"])
                rsqrt(tmp_s[:, 0:TT], psQ[:, 0:TT], ["psQ"], ["tmp_s"], bias=EPS, scale=1.0 / 128)
                stt("dve", mixT[:, 4 + g, :], y, pp[:, PP_SGOG + g:PP_SGOG + g + 1], tmp_s[:, 0:TT], ALU.mult, ALU.mult,
                    ["gt1", "pp", "tmp_s"], ["mixT_sg"])
                step()
            if first:
                dump("ysg", mixT[:, 4:8, :], [128, 4, TT], BF16, ["mixT_sg"])
            dma("sync", scr_ysg[tt].rearrange("p (c t) -> p c t", c=4), mixT[:, 4:8, :], ["mixT_sg"], ["scr_ysg%d" % tt], "st_ysg")

        def delta1_gen(tt):
            yield from delta_tile(tt, 0)
            if tt == 0:
                dump("o1T", o1T[:], [128, 4, TT], BF16, ["o1T"])
            dma("sync", scr_o1[tt].rearrange("p (c t) -> p c t", c=4), o1T[:], ["o1T"], ["scr_o1%d" % tt], "st_o1")

        PSE0 = ["psE0"]
        PSE1 = ["psE1"]
        PSE = PSE0 + PSE1

        def delta_tile(tt, r):
            qb = tt % 2
            tt_("dve", t16[:], ab_tm2[:, qb], bp[:, BP_SGN:BP_SGN + 16].unsqueeze(1).to_broadcast([128, 4, 16]), ALU.mult,
                ["ab_tm%d" % qb, "bp"], ["t16"])
            tt_("dve", t16[:], t16[:], bp[:, BP_OFF:BP_OFF + 16].unsqueeze(1).to_broadcast([128, 4, 16]), ALU.add,
                ["t16", "bp"], ["t16"])
            act(t16[:], t16[:], AF.Exp, ["t16"], ["t16"])
            act(t16[:], t16[:], AF.Ln, ["t16"], ["t16"], bias=1.0)
            tt_("dve", G16[:], t16[:], mul16[:].unsqueeze(1).to_broadcast([128, 4, 16]), ALU.mult, ["t16", "mul16"], ["G16"])
            if tt == 0 and r == 0:
                dump("G16", G16[:], [128, 4, 16], F32, ["G16"])
            yield
            blocks = range(4) if r == 0 else range(3, -1, -1)
            for b in blocks:
                yield from delta_block(tt, b, r)

        def delta_block(tt, b, r):
            qb = tt % 2
            qkvT = qkvT2[qb]
            qk_keys = qkeys(qb)
            c0 = b * 128
            g4 = G16[:, b, r * 4:r * 4 + 4]
            lb4 = G16[:, b, 8 + r * 4:12 + r * 4]
            dbg0 = (tt == 0 and b == 0 and r == 0)
            mm(psS[:, 96:100], Mc[r], g4, True, True, ["cs", "G16"], ["psTS"])
            mm(psS[:, 100:104], BD, g4, True, True, ["cs", "G16"], ["psTS"])
            cp("dve", sc12[:, 0:4], psS[:, 96:100], ["psTS"], ["sc12"])
            tt_("dve", sc12[:, 4:8], psS[:, 100:104], sc12[:, 0:4], ALU.subtract, ["psTS", "sc12"], ["sc12"])
            cp("dve", sc12[:, 8:12], lb4, ["G16", "sc12"], ["sc12"])
            act(esc[:], sc12[:], AF.Exp, ["sc12"], ["esc"])
            ts_("dve", ngc[:], sc12[:, 0:4], -1.0, None, ALU.mult, None, ["sc12"], ["ngc"])
            ts_("dve", nbe[:], esc[:, 8:12], -1.0, None, ALU.mult, None, ["esc"], ["nbe"])
            yield
            tt_("dve", rhsP[:], Mc[r].unsqueeze(1).to_broadcast([128, 4, 128]), g4.unsqueeze(2).to_broadcast([128, 4, 128]),
                ALU.mult, ["cs", "G16"], ["rhsP"])
            psD3 = psD[:, 0:512].rearrange("p (h i) -> p h i", h=4)
            mm(psD[:, 0:512], ones_f[:], rhsP[:].rearrange("p h i -> p (h i)"), True, True, ["ones_f", "rhsP"], ["psD"])
            act(EGb[:], psD3, AF.Exp, ["psD"], ["EGb"])
            psG3 = psG[:, 0:512].rearrange("p (h i) -> p h i", h=4)
            mm(psG[:, 0:512], ones_f[:], rhsP[:].rearrange("p h i -> p (h i)"), True, False, ["ones_f", "rhsP"], ["psG"])
            mm(psG[:, 0:512], ident_b[:], negm_b[:, r, :], False, True, ["ident_b", "negm_b"], ["psG"])
            for h in range(4):
                act(decT[:, h, :], psG3[:, h, :], AF.Exp, ["psG", "ngc"], ["decT"], bias=ngc[:, h:h + 1])
            if dbg0:
                dump("decT", decT[:], [128, 4, 128], F32, ["decT"])
                dump("EGb", EGb[:], [128, 4, 128], F32, ["EGb"])
            yield
            stt("dve", qgT[:], qkvT[:, 0:4, c0:c0 + 128], QSCALE, EGb[:], ALU.mult, ALU.mult, qk_keys + ["EGb"], ["qgT"])
            for h in range(4):
                tr(psT[:, h, :], qkvT[:, 4 + h, c0:c0 + 128], qk_keys, ["psTS"])
            tt_("dve", kg[:], psT[:], esc[:, 0:4].unsqueeze(2).to_broadcast([128, 4, 128]), ALU.mult, ["psTS", "esc"], ["kg"])
            tt_("dve", ktl[:], psT[:], esc[:, 4:8].unsqueeze(2).to_broadcast([128, 4, 128]), ALU.mult, ["psTS", "esc"], ["ktl"])
            for h in range(4):
                tr(psT[:, h, :], qkvT[:, 8 + h, c0:c0 + 128], qk_keys, ["psTS"])
            cp("act", vtm[:], psT[:], ["psTS"], ["vtm"])
            yield
            psE4 = psE[:, :].rearrange("p (h two i) -> p h two i", h=4, two=2)
            for h in range(4):
                mm(psE4[:, h, 0, :], qkvT[:, 4 + h, c0:c0 + 128], qkvT[:, 4 + h, c0:c0 + 128], True, True, qk_keys, PSE)
                mm(psE4[:, h, 1, :], qkvT[:, 4 + h, c0:c0 + 128], qkvT[:, h, c0:c0 + 128], True, True, qk_keys, PSE)
            tt_("dve", Ebs[:], decT[:], esc[:, 8:12].unsqueeze(2).to_broadcast([128, 4, 128]), ALU.mult, ["decT", "esc"], ["Ebs"])
            tt_("dve", Ebs[:], Ebs[:], st01[:, r, :].rearrange("p (h i) -> p h i", h=4), ALU.mult, ["Ebs", "st01"], ["Ebs"])
            tt_("dve", Nm[:], psE4[:, :, 0, :], Ebs[:], ALU.mult, PSE + ["Ebs"], ["Nm"])
            tt_("dve", Pa[0][:], ident_b[:].unsqueeze(1).to_broadcast([128, 4, 128]), Nm[:], ALU.subtract, ["ident_b", "Nm"], ["Pa0"])
            stt("dve", attT[:], psE4[:, :, 1, :], QSCALE, decT[:], ALU.mult, ALU.mult, PSE + ["decT"], ["attT"])
            if dbg0:
                dump("Nm", Nm[:], [128, 4, 128], BF16, ["Nm"])
                dump("attT", attT[:], [128, 4, 128], BF16, ["attT"])
            yield
            for h in range(4):
                tr(psT[:, h, :], Nm[:, h, :], ["Nm"], ["psTS"])
            cp("act", NmT[:], psT[:], ["psTS"], ["NmT"])
            X, Xk, XT, XTk = Nm, "Nm", NmT, "NmT"
            pcur = 0
            psG3 = psG[:, 0:512].rearrange("p (h i) -> p h i", h=4)
            psE0_3 = psE[:, 0:512].rearrange("p (h i) -> p h i", h=4)
            psE1_3 = psE[:, 512:1024].rearrange("p (h i) -> p h i", h=4)
            for lvl in range(5):
                s = lvl % 2
                last = lvl == 4
                for h in range(4):
                    mm(psE0_3[:, h, :], X[:, h, :], XT[:, h, :], True, True, [Xk, XTk], PSE0)
                if not last:
                    for h in range(4):
                        mm(psG3[:, h, :], XT[:, h, :], X[:, h, :], True, True, [Xk, XTk], ["psG"])
                cp("act", XTa[s][:], psE0_3, PSE0, ["XTa%d" % s])
                if not last:
                    cp("dve", Xa[s][:], psG3, ["psG"], ["Xa%d" % s])
                yield
                X, Xk, XT, XTk = Xa[s], "Xa%d" % s, XTa[s], "XTa%d" % s
                pk0, pk1 = "Pa%d" % pcur, "Pa%d" % (1 - pcur)
                for h in range(4):
                    mm(psE1_3[:, h, :], XT[:, h, :], Pa[pcur][:, h, :], True, True, [XTk, pk0], PSE1)
                tt_("dve", Pa[1 - pcur][:], psE1_3, Pa[pcur][:], ALU.add, PSE1 + [pk0], [pk1])
                pcur = 1 - pcur
                yield
            Tt, Tk = Pa[pcur], "Pa%d" % pcur
            if dbg0:
                dump("Tt", Tt[:], [128, 4, 128], BF16, [Tk])
            for h in range(4):
                mm(psG3[:, h, :], Tt[:, h, :], vtm[:, h, :], True, True, [Tk, "vtm"], ["psG"])
            for h in range(4):
                mm(psE0_3[:, h, :], kg[:, h, :], Tt[:, h, :], True, True, [Tk, "kg"], PSE0)
            tt_("dve", ub[:], psG3, esc[:, 8:12].unsqueeze(2).to_broadcast([128, 4, 128]), ALU.mult, ["psG", "esc"], ["ub"])
            cp("act", wT[:], psE0_3, PSE0, ["wT"])
            yield
            chunks = (0, 1) if r == 0 else (1, 0)
            for ci in chunks:
                R = slice(64 * ci, 64 * ci + 64)
                lastcol = (64 * ci + 63) if r == 0 else (64 * ci)
                for h in range(4):
                    mm(psE0_3[:, h, :], wT[:, h, :], S_b[:, h, :], True, True, ["wT", "S_b%d" % h], ["psE0"])
                yield
                for h in range(4):
                    stt("dve", vnew[R, h, :], psE0_3[R, h, :], nbe[R, h:h + 1], ub[R, h, :], ALU.mult, ALU.add,
                        ["psE0", "nbe", "ub"], ["vnew%d" % h])
                for h in range(4):
                    mm(psD3[:, h, R], S_b[:, h, :], qgT[:, h, R], True, False, ["S_b%d" % h, "qgT"], ["psD"])
                    mm(psD3[:, h, R], vnew[R, h, :], attT[R, h, R], False, True, ["vnew%d" % h, "attT"], ["psD"])
                    mm(psE1_3[:, h, :], ktl[R, h, :], vnew[R, h, :], True, True, ["ktl", "vnew%d" % h], ["psE1"])
                for h in range(4):
                    stt("dve", S_f[:, h, :], S_f[:, h, :], EGb[:, h, lastcol:lastcol + 1], psE1_3[:, h, :], ALU.mult, ALU.add,
                        ["S_f%d" % h, "EGb", "psE1"], ["S_f%d" % h])
                    cp("act", S_b[:, h, :], S_f[:, h, :], ["S_f%d" % h], ["S_b%d" % h])
                yield
            if r == 0:
                cp("act", o1T[:, :, c0:c0 + 128], psD3, ["psD"], ["o1T"])
            else:
                tt_("dve", oacc[:, :, c0:c0 + 128], psD3, o1T[:, :, c0:c0 + 128], ALU.add, ["psD", "o1T"], ["oacc"])

        def sweep2_pre(tt):
            qb = tt % 2
            dma("sync", qkvT2[qb][:], scr_qkv[tt].rearrange("p (c t) -> p c t", c=12), ["scr_qkv%d" % tt], qkeys(qb), "ld_qkv%d" % qb)
            dma("sync", o1T[:], scr_o1[tt].rearrange("p (c t) -> p c t", c=4), ["scr_o1%d" % tt], ["o1T"], "ld_o1")
            dma("sync", ab_tm2[:, qb], scr_ab[tt].rearrange("p (b c) -> p b c", b=4), ["scr_ab%d" % tt], ["ab_tm%d" % qb], "ld_ab%d" % qb)
            return delta_tile(tt, 1)

        def sweep2_tile(l, tt, xsrc, xdst, lastlayer, nxt):
            t0 = tt * TT

            step = mk_step(nxt, 1.0)

            dma("sync", szT[:], scr_sz[tt].rearrange("p (c t) -> p c t", c=4), ["scr_sz%d" % tt], ["szT"], "ld_sz")
            dma("sync", mixT[:, 4:8, :], scr_ysg[tt].rearrange("p (c t) -> p c t", c=4), ["scr_ysg%d" % tt], ["mixT_sg"], "ld_ysg")
            load_x(xsrc, tt, False)
            if tt == 0:
                dump("oacc", oacc[:], [128, 4, TT], F32, ["oacc"])
            for h in range(4):
                act(sqs[:, 0, 0:TT], oacc[:, h, :], AF.Square, ["oacc"], ["sqs0"])
                mm(psQ[:, 0:TT], ones_b[:], sqs[:, 0, 0:TT], True, True, ["ones_b", "sqs0"], ["psQ"])
                rsqrt(tmp_s[:, 0:TT], psQ[:, 0:TT], ["psQ"], ["tmp_s"], bias=EPS, scale=1.0 / 128)
                stt("dve", tmp_s[:, 0:TT], oacc[:, h, :], pp[:, PP_DNNG:PP_DNNG + 1], tmp_s[:, 0:TT], ALU.mult, ALU.mult,
                    ["oacc", "pp", "tmp_s"], ["tmp_s"])
                tt_("dve", mixT[:, h, :], tmp_s[:, 0:TT], szT[:, h, :], ALU.mult, ["tmp_s", "szT"], ["mixT_dn"])
                step()
            if tt == 0:
                dump("mixT", mixT[:], [128, 8, TT], BF16, ["mixT_dn", "mixT_sg"])
            for g in range(2):
                wv, wk = w_next("out", l, g * 512)
                for mi in range(4):
                    m = g * 4 + mi
                    ps, pk = dense_ps()
                    for k in range(KD):
                        mm(ps[:, 0:TT], wv[:, k, mi * 128:(mi + 1) * 128], mixT[:, k, :], k == 0, k == KD - 1,
                           [wk, "mixT_dn", "mixT_sg"], [pk])
                    tt_("dve", xt[:, m, 0:TT], xt[:, m, 0:TT], ps[:, 0:TT], ALU.add, ["xt", pk], ["xt"])
                    step()
            if tt == 0:
                dump("xmid", xt[:, :, 0:TT], [128, KD, TT], F32, ["xt"])
            norm_stage(PP_GFFN, False, False)
            for g in range(6):
                n = 4 if g < 5 else 2
                wg, wgk = w_next("gate", l, g * 512)
                wu, wuk = w_next("up", l, g * 512)
                for mi in range(n):
                    m = g * 4 + mi
                    pg, pgk = dense_ps()
                    pu, puk = dense_ps()
                    for k in range(KD):
                        mm(pg[:, 0:TT], wg[:, k, mi * 128:(mi + 1) * 128], hT[:, k, 0:TT], k == 0, k == KD - 1, [wgk, "hT"], [pgk])
                    for k in range(KD):
                        mm(pu[:, 0:TT], wu[:, k, mi * 128:(mi + 1) * 128], hT[:, k, 0:TT], k == 0, k == KD - 1, [wuk, "hT"], [puk])
                    s = m % 2
                    act(cv[:, s, :], pg[:, 0:TT], AF.Silu, [pgk], ["cv%d" % s])
                    tt_("dve", hidT[:, m, :], cv[:, s, :], pu[:, 0:TT], ALU.mult, ["cv%d" % s, puk], ["hidT"])
                    step()
            for m in range(8):
                wv, wk = w_next("down", l, m * 128)
                ps, pk = dense_ps()
                for k in range(KF):
                    mm(ps[:, 0:TT], wv[:, k, :], hidT[:, k, :], k == 0, k == KF - 1, [wk, "hidT"], [pk])
                tt_("dve", xt[:, m, 0:TT], xt[:, m, 0:TT], ps[:, 0:TT], ALU.add, ["xt", pk], ["xt"])
                step()
            if tt == 0:
                dump("hid", hidT[:], [128, KF, TT], BF16, ["hidT"])
                dump("xfin", xt[:, :, 0:TT], [128, KD, TT], F32, ["xt"])
            if not lastlayer:
                dma("sync", xdst[0][:, :, t0:t0 + TT], xt[:, :, 0:TT], ["xt"], ["%s_%d" % (xdst[1], tt)], "st_x")
            else:
                for k in range(KD):
                    s = k % 2
                    act(sqs[:, s, 0:TT], xt[:, k, 0:TT], AF.Square, ["xt"], ["sqs%d" % s])
                    mm(psQ[:, 0:TT], ones_b[:], sqs[:, s, 0:TT], k == 0, k == KD - 1, ["ones_b", "sqs%d" % s], ["psQ"])
                rsqrt(rstd_b[:, 0:TT], psQ[:, 0:TT], ["psQ"], ["rstd_b"], bias=1024 * EPS)
                for k in range(KD):
                    s = k % 2
                    stt("dve", cv[:, s, :], xt[:, k, 0:TT], fg[:, k:k + 1], rstd_b[:, 0:TT], ALU.mult, ALU.mult,
                        ["xt", "fg", "rstd_b"], ["cv%d" % s])
                    dma("sync", out_d[:, k, t0:t0 + TT], cv[:, s, :], ["cv%d" % s], ["out%d" % k], "st_out%d" % s)

        for l in range(L):
            load_layer_params(l)
            xsrc = (xin, "xin") if l == 0 else (xs1, "xs1")
            xdst = (xs1, "xs1")
            lastlayer = l == L - 1
            for h in range(4):
                memset("dve", S_f[:, h, :], 0.0, ["S_f%d" % h])
                memset("dve", S_b[:, h, :], 0.0, ["S_b%d" % h])
            P.op("dve", lambda h: h.memset(dummy[:], 0.0), [], ["hidT"] + ["pbuf%d" % j for j in range(12)])
            proj_tile(l, 0, xsrc, None)
            for tt in range(NTT):
                g1 = delta1_gen(tt)
                if tt + 1 < NTT:
                    proj_tile(l, tt + 1, xsrc, g1)
                for _ in g1:
                    pass
            for h in range(4):
                memset("dve", S_f[:, h, :], 0.0, ["S_f%d" % h])
                memset("dve", S_b[:, h, :], 0.0, ["S_b%d" % h])
            P.op("dve", lambda h: h.memset(dummy[:], 0.0), [], ["hidT"] + ["pbuf%d" % j for j in range(12)])
            gen = sweep2_pre(NTT - 1)
            for _ in gen:
                pass
            for tt in range(NTT - 1, -1, -1):
                nxt = sweep2_pre(tt - 1) if tt > 0 else None
                sweep2_tile(l, tt, xsrc, xdst, lastlayer, nxt)
                if nxt is not None:
                    for _ in nxt:
                        pass
        P.op("sync", lambda h: h.nop(), ["out%d" % k for k in range(KD)] + ["dbg_" + n for n in dbg_outs], [])
        P.emit(st)
    return nc


def _consts():
    i = np.arange(128)
    same = (i[:, None] // 64) == (i[None, :] // 64)
    c = np.zeros((128, CS_N), np.float32)
    c[:, CS_ID:CS_ID + 128] = np.eye(128)
    c[:, CS_MC0:CS_MC0 + 128] = same & (i[:, None] <= i[None, :])
    c[:, CS_MC1:CS_MC1 + 128] = same & (i[:, None] >= i[None, :])
    c[:, CS_BD:CS_BD + 128] = same
    c[:, CS_NM0:CS_NM0 + 128] = np.where(same & (i[None, :] >= i[:, None]), 0.0, -30000.0)
    c[:, CS_NM1:CS_NM1 + 128] = np.where(same & (i[None, :] <= i[:, None]), 0.0, -30000.0)
    c[:, CS_ST0:CS_ST0 + 128] = same & (i[None, :] > i[:, None])
    c[:, CS_ST1:CS_ST1 + 128] = same & (i[None, :] < i[:, None])
    return c


def _host_params(inp, L):
    f = np.float32
    pp = np.zeros((128, L, PP_N), f)
    bp = np.zeros((128, L, BP_N), f)
    sgwT = np.zeros((L, 128, 512), f)
    for l in range(L):
        pp[:, l, PP_GMIX:PP_GMIX + 8] = np.asarray(inp["mix_norm_g"][l]).reshape(8, 128).T
        pp[:, l, PP_GFFN:PP_GFFN + 8] = np.asarray(inp["ffn_norm_g"][l]).reshape(8, 128).T
        cw = np.asarray(inp["conv_w"][l])
        pp[:, l, PP_CONV:PP_CONV + 36] = cw.reshape(3, 12, 128).transpose(2, 1, 0).reshape(128, 36)
        pp[:, l, PP_DNNG] = np.asarray(inp["dn_norm_g"][l])
        pp[:, l, PP_SGOG:PP_SGOG + 4] = np.asarray(inp["sg_out_g"][l]).reshape(4, 128).T
        bp[:, l, BP_ALOG:BP_ALOG + 8] = np.asarray(inp["dn_a_log"][l]).reshape(8)[None, :]
        bp[:, l, BP_OFF:BP_OFF + 8] = np.asarray(inp["dn_dt_bias"][l]).reshape(8)[None, :]
        bp[:, l, BP_SGN:BP_SGN + 8] = 1.0
        bp[:, l, BP_SGN + 8:BP_SGN + 16] = -1.0
        bp[:, l, BP_LNG:BP_LNG + 512] = np.asarray(inp["sg_ln_g"][l])[None, :]
        bp[:, l, BP_LNB:BP_LNB + 512] = np.asarray(inp["sg_ln_b"][l])[None, :]
        bp[:, l, BP_BSB:BP_BSB + 512] = np.asarray(inp["sg_b"][l]).reshape(512)[None, :]
        sgwT[l] = np.asarray(inp["sg_w"][l]).transpose(2, 0, 1).reshape(128, 512)
    fgv = np.ascontiguousarray(np.asarray(inp["final_norm_g"]).reshape(8, 128).T).astype(f)
    return pp, bp, sgwT, fgv


_NC_CACHE = {}


def run(inp, dbg=None, trace=False):
    x = np.asarray(inp["x"])
    B, S, _ = x.shape
    L = np.asarray(inp["w_in"]).shape[0]
    key = (S, L, tuple(sorted(dbg or ())))
    if key not in _NC_CACHE:
        _NC_CACHE[key] = build(S, L, dbg)
    nc = _NC_CACHE[key]
    pp, bp, sgwT, fgv = _host_params(inp, L)
    cs = _consts()
    shared = {
        "w_in": np.ascontiguousarray(inp["w_in"], np.float32), "w_out": np.ascontiguousarray(inp["w_out"], np.float32),
        "w_gate": np.ascontiguousarray(inp["w_gate"], np.float32), "w_up": np.ascontiguousarray(inp["w_up"], np.float32),
        "w_down": np.ascontiguousarray(inp["w_down"], np.float32),
        "pp": pp, "fg": fgv, "bp": bp, "sgwT": sgwT, "consts": cs,
    }
    in_maps = []
    for b in range(B):
        xi = np.zeros((128, KD, S + 1), np.float32)
        xi[:, :, 0:S] = x[b].reshape(S, KD, 128).transpose(2, 1, 0)
        m = dict(shared)
        m["xin"] = xi
        in_maps.append(m)
    res = run_bass_kernel_spmd(nc, in_maps, core_ids=list(range(B)), **({"trace": True} if trace else {}))
    out = np.zeros((B, S, D), np.float32)
    for b in range(B):
        out[b] = res.results[b]["out"].transpose(2, 1, 0).reshape(S, D)
    return out, res


def kernel(**inputs):
    out, _ = run(inputs)
    return out
```
